# Optimizing a Trainium2 kernel written in Bass

```python
import jax, jax.numpy as jnp
from jax import lax
import numpy as np

D_MODEL = 2048
BATCH = 4
SEQ = 2048
DEPTH = 4
DEC_BATCH = 128
DEC_SEQ = 1
PAST_LEN = 16384
PAGE_SIZE = 128

D_MIX = D_MODEL
CONV_WIDTH = D_MIX // 2
RET_WIDTH = D_MIX - CONV_WIDTH
CONV_K = 3
RET_HEADS = 4
RET_HEAD_DIM = RET_WIDTH // RET_HEADS
RET_CHUNK = 128
ROPE_BASE = 10000.0
PLE_DIM = 256
NORM_EPS = 1e-6
GN_EPS = 1e-6
IN_COLS = 4 * CONV_WIDTH + 4 * RET_WIDTH

kernel_name = "hybrid_conv_retention_decoder_step"


def rmsnorm(x, g):
    xf = x.astype(jnp.float32)
    y = xf * lax.rsqrt(jnp.mean(xf * xf, axis=-1, keepdims=True) + NORM_EPS)
    return (y * g.astype(jnp.float32)).astype(x.dtype)


def rope(x, pos):
    half = x.shape[-1] // 2
    inv = jnp.power(ROPE_BASE, -jnp.arange(half, dtype=jnp.float32) / half)
    ang = pos.astype(jnp.float32)[:, None] * inv[None, :]
    cos = jnp.cos(ang)[None, :, None, :]
    sin = jnp.sin(ang)[None, :, None, :]
    xf = x.astype(jnp.float32)
    x1, x2 = xf[..., :half], xf[..., half:]
    return jnp.concatenate([x1 * cos - x2 * sin, x1 * sin + x2 * cos], axis=-1)


def retention_log_gamma():
    return jnp.log1p(-jnp.exp2(-5.0 - jnp.arange(RET_HEADS, dtype=jnp.float32)))


def retention_chunk(S, q, k, v, log_gamma):
    L = q.shape[1]
    idx = jnp.arange(L, dtype=jnp.float32)
    diff = idx[:, None] - idx[None, :]
    decay = jnp.where(diff >= 0.0,
                      jnp.exp(log_gamma[:, None, None] * jnp.maximum(diff, 0.0)[None]),
                      0.0)
    scores = jnp.einsum('blhd,bmhd->bhlm', q, k) * decay[None]
    intra = jnp.einsum('bhlm,bmhe->blhe', scores, v)
    q_dec = jnp.exp(log_gamma[None, :] * (idx[:, None] + 1.0))
    inter = jnp.einsum('blhd,bhde->blhe', q, S) * q_dec[None, :, :, None]
    k_dec = jnp.exp(log_gamma[None, :] * (L - 1.0 - idx[:, None]))
    S_new = (jnp.exp(log_gamma * L)[None, :, None, None] * S
             + jnp.einsum('blhd,blhe->bhde', k * k_dec[None, :, :, None], v))
    return S_new, intra + inter


def retention(q, k, v, S0, log_gamma):
    B, L, H, Dk = q.shape
    C = RET_CHUNK if L % RET_CHUNK == 0 else L
    n = L // C

    def to_chunks(t):
        return t.reshape(B, n, C, H, t.shape[-1]).transpose(1, 0, 2, 3, 4)

    def step(S, qkv):
        qc, kc, vc = qkv
        return retention_chunk(S, qc, kc, vc, log_gamma)

    S_final, o = lax.scan(step, S0, (to_chunks(q), to_chunks(k), to_chunks(v)))
    o = o.transpose(1, 0, 2, 3, 4).reshape(B, L, H, v.shape[-1])
    return o, S_final


def short_conv(u, buf, w):
    L = u.shape[1]
    ext = jnp.concatenate([buf.astype(u.dtype), u], axis=1)
    y = w[0] * ext[:, 0:L]
    for j in range(1, CONV_K):
        y = y + w[j] * ext[:, j:j + L]
    return y, ext[:, -(CONV_K - 1):]


def split_projection(z):
    widths = [CONV_WIDTH] * 4 + [RET_WIDTH] * 4
    points = [int(s) for s in np.cumsum(widths)[:-1]]
    return jnp.split(z, points, axis=-1)


def trunk_layer(x, p_i, pos, conv_buf, ret_S, norm_g, w_in, conv_w, gn_g, w_out, w_pg, w_ple):
    Bsz, L, _ = x.shape
    h = rmsnorm(x, norm_g)
    z = h @ w_in
    b_c, c_c, x_c, g_c, q, k, v, g_r = split_projection(z)
    conv_y, new_buf = short_conv(c_c * x_c, conv_buf, conv_w)
    y_conv = b_c * conv_y * jax.nn.silu(g_c)
    q = rope(q.reshape(Bsz, L, RET_HEADS, RET_HEAD_DIM), pos)
    k = rope(k.reshape(Bsz, L, RET_HEADS, RET_HEAD_DIM), pos) * (RET_HEAD_DIM ** -0.5)
    v = v.reshape(Bsz, L, RET_HEADS, RET_HEAD_DIM).astype(jnp.float32)
    o, new_S = retention(q, k, v, ret_S.astype(jnp.float32), retention_log_gamma())
    mu = jnp.mean(o, axis=-1, keepdims=True)
    var = jnp.mean(jnp.square(o - mu), axis=-1, keepdims=True)
    o = ((o - mu) * lax.rsqrt(var + GN_EPS)).reshape(Bsz, L, RET_WIDTH)
    y_ret = (o * gn_g.astype(jnp.float32)).astype(x.dtype) * jax.nn.silu(g_r)
    x = x + jnp.concatenate([y_conv, y_ret], axis=-1) @ w_out
    x = x + jax.nn.sigmoid(x @ w_pg) * (p_i @ w_ple)
    return x, new_buf, new_S.astype(x.dtype)


def setup_inputs(seed: int = 0) -> dict:
    key = jax.random.key(seed)
    ks = jax.random.split(key, 16)
    f32 = jnp.float32
    return {
        "x_prompt": jax.random.normal(ks[0], (BATCH, SEQ, D_MODEL), f32),
        "x_sample": jax.random.normal(ks[1], (DEC_BATCH, DEC_SEQ, D_MODEL), f32),
        "state_conv": jax.random.normal(ks[2], (DEPTH, DEC_BATCH, CONV_K - 1, CONV_WIDTH), f32),
        "state_ret": 0.5 * jax.random.normal(ks[3], (DEPTH, DEC_BATCH, RET_HEADS, RET_HEAD_DIM, RET_HEAD_DIM), f32),
        "p_prompt": jax.random.normal(ks[4], (DEPTH, BATCH, SEQ, PLE_DIM), f32),
        "p_sample": jax.random.normal(ks[5], (DEPTH, DEC_BATCH, DEC_SEQ, PLE_DIM), f32),
        "norm_g": 1.0 + 0.02 * jax.random.normal(ks[6], (DEPTH, D_MODEL), f32),
        "w_in": jax.random.normal(ks[7], (DEPTH, D_MODEL, IN_COLS), f32) * D_MODEL ** -0.5,
        "conv_w": jax.random.normal(ks[8], (DEPTH, CONV_K, CONV_WIDTH), f32) * CONV_K ** -0.5,
        "gn_g": 1.0 + 0.02 * jax.random.normal(ks[9], (DEPTH, RET_WIDTH), f32),
        "w_out": jax.random.normal(ks[10], (DEPTH, D_MIX, D_MODEL), f32) * D_MIX ** -0.5,
        "w_pg": jax.random.normal(ks[11], (DEPTH, D_MODEL, D_MODEL), f32) * D_MODEL ** -0.5,
        "w_ple": jax.random.normal(ks[12], (DEPTH, PLE_DIM, D_MODEL), f32) * PLE_DIM ** -0.5,
        "final_g": 1.0 + 0.02 * jax.random.normal(ks[13], (D_MODEL,), f32),
    }


def reference(x_prompt, x_sample, state_conv, state_ret, p_prompt, p_sample,
              norm_g, w_in, conv_w, gn_g, w_out, w_pg, w_ple, final_g):
    pos_prompt = jnp.arange(SEQ, dtype=jnp.int32)
    pos_sample = PAST_LEN + jnp.arange(DEC_SEQ, dtype=jnp.int32)
    hp, hs = x_prompt, x_sample
    conv_p, ret_p, conv_s, ret_s = [], [], [], []
    for i in range(DEPTH):
        buf0 = jnp.zeros((BATCH, CONV_K - 1, CONV_WIDTH), x_prompt.dtype)
        S0 = jnp.zeros((BATCH, RET_HEADS, RET_HEAD_DIM, RET_HEAD_DIM), jnp.float32)
        hp, nb, nS = trunk_layer(hp, p_prompt[i], pos_prompt, buf0, S0,
                                 norm_g[i], w_in[i], conv_w[i], gn_g[i], w_out[i], w_pg[i], w_ple[i])
        conv_p.append(nb)
        ret_p.append(nS)
        hs, nb, nS = trunk_layer(hs, p_sample[i], pos_sample, state_conv[i], state_ret[i],
                                 norm_g[i], w_in[i], conv_w[i], gn_g[i], w_out[i], w_pg[i], w_ple[i])
        conv_s.append(nb)
        ret_s.append(nS)
    y_prompt = rmsnorm(hp, final_g)
    y_sample = rmsnorm(hs, final_g)
    return (y_prompt, y_sample, jnp.stack(conv_p), jnp.stack(ret_p), jnp.stack(conv_s), jnp.stack(ret_s))
```

```python
import numpy as np
from contextlib import ExitStack
import concourse.bass as bass
import concourse.mybir as mybir
from concourse.bass_utils import run_bass_kernel_spmd

F32 = mybir.dt.float32
BF16 = mybir.dt.bfloat16
ALU = mybir.AluOpType
AF = mybir.ActivationFunctionType

D = 2048
L = 4
SEQ = 2048
NPR = 512
NS = 16
NPASS = 4
TMAX = NPR + NS
NCORE = 8
GAM = [1.0 - 2.0 ** (-5 - h) for h in range(4)]
G128 = [g ** 128 for g in GAM]
WSLOT = 16 * 256 + 2 * 256
NWS = 3
CFG = {"stage": 99}
CACHE_PLAN = [0, 0, 0, 0, 1, 2]
SAMPLE_LAST = True
PRE_NORM = True
PMAP = {0: 0, 1: 1, 4: 2, 5: 3}


class _Rec:
    def __getattr__(self, name):
        def f(*a, **k):
            return (name, a, k)
        return f


_REC = _Rec()


class Plan:
    def __init__(self):
        self.engs = ["pe", "act", "dve", "pool", "sp"]
        self.lists = {e: [] for e in self.engs}
        self.count = {e: 0 for e in self.engs}
        self.seen = {e: {} for e in self.engs}
        self.lastw = {}
        self.readers = {}
        self.nslot = {"sp": 20, "pool": 12}
        self.slot_uses = {q: [0] * n for q, n in self.nslot.items()}
        self.slot_next = {q: 0 for q in self.nslot}

    def _wait(self, eng, st):
        if st is None:
            return
        key, val = st
        if eng == "pe" and key == "pe":
            return
        if self.seen[eng].get(key, 0) >= val:
            return
        self.seen[eng][key] = val
        self.lists[eng].append(("wait", key, val))

    def _deps(self, eng, R, W):
        for r in R:
            self._wait(eng, self.lastw.get(r))
        for w in W:
            self._wait(eng, self.lastw.get(w))
            for st in self.readers.get(w, ()):
                self._wait(eng, st)

    def _record(self, stamp, R, W):
        for r in R:
            self.readers.setdefault(r, []).append(stamp)
        for w in W:
            self.lastw[w] = stamp
            self.readers[w] = []

    def op(self, eng, fn, R=(), W=()):
        W = list(W) + [r for r in R if isinstance(r, tuple) and r[0] == "ps"]
        self._deps(eng, R, W)
        self.count[eng] += 1
        stamp = (eng, self.count[eng])
        self.lists[eng].append(("op", fn(_REC)))
        self._record(stamp, R, W)

    def dma(self, q, fn, R=(), W=()):
        i = self.slot_next[q]
        self.slot_next[q] = (i + 1) % self.nslot[q]
        uses = self.slot_uses[q][i]
        key = ("dma", q, i)
        if uses > 0:
            self._wait(q, (key, 16 * uses))
        self._deps(q, R, W)
        self.slot_uses[q][i] += 1
        stamp = (key, 16 * (uses + 1))
        self.lists[q].append(("dma", fn(_REC), key))
        self._record(stamp, R, W)


def build_nc():
    nc = bass.Bass("TRN2", target_bir_lowering=False)

    def din(name, shape):
        return nc.dram_tensor(name, list(shape), F32, kind="ExternalInput").ap()

    def dout(name, shape):
        return nc.dram_tensor(name, list(shape), F32, kind="ExternalOutput").ap()

    xp = din("xp", [128, 16, SEQ])
    xs = din("xs", [128, 16, NS])
    pp = din("pp", [L, 128, 2, SEQ])
    psm = din("psm", [L, 128, 2, NS])
    scT = din("scT", [L, 128, 8, 2, NS])
    sret = din("sret", [L, NS, 4, 256, 256])
    w_in = din("w_in", [L, D, 8192])
    w_out = din("w_out", [L, D, D])
    w_pg = din("w_pg", [L, D, D])
    w_ple = din("w_ple", [L, 256, D])
    ng_d = din("ng", [128, L, 16])
    fg_d = din("fg", [128, 16])
    cw_d = din("cw", [128, L, 8, 3])
    gg_d = din("gg", [128, L, 8])
    cs_d = din("cs", [128, 2, NPASS, TMAX])
    dq_d = din("dq", [128, 4, 128])
    dk_d = din("dk", [128, 4, 128])
    mask_d = din("maskT", [128, 128])
    c3_d = din("c3", [128, 3, 128])
    eyep_d = din("eyep", [16, 16])
    eyeg_d = din("eyeg", [128, 16, 16])

    yT = dout("yT", [128, 16, SEQ])
    ysT = dout("ysT", [128, 16, NS])
    ncp = dout("ncp", [L, 128, 8, 2])
    nrp = dout("nrp", [L, 128, 8, 256])
    ncs = dout("ncs", [L, 128, 8, 2, NS])
    nrs = dout("nrs", [L, NS, 4, 256, 256])

    P = Plan()
    with ExitStack() as es:
        def sb(name, shape, dt=F32):
            return es.enter_context(nc.sbuf_tensor(name, list(shape), dt))

        xT = sb("xT", [128, 16, TMAX])
        hT = sb("hT", [128, 16, TMAX], BF16)
        ymT = sb("ymT", [128, 16, TMAX], BF16)
        wb = [sb(f"wb{i}", [128, WSLOT], BF16) for i in range(NWS)]
        qT = sb("qT", [128, 8, TMAX], BF16)
        kT = sb("kT", [128, 8, TMAX], BF16)
        v_tok = sb("v_tok", [128, 5, 1024], BF16)
        k_tok = [sb(f"k_tok{i}", [128, 1024], BF16) for i in range(2)]
        csT = sb("csT", [128, 2, TMAX])
        raw = sb("raw", [128, 2, TMAX])
        tb = sb("tb", [128, TMAX])
        tc = sb("tc", [128, TMAX])
        uext = sb("uext", [128, TMAX + 2])
        acc = sb("acc", [128, TMAX])
        sg = sb("sg", [128, TMAX])
        rt = [tb, tc, acc, sg]
        RTK = ["tb", "tc", "acc", "sg"]
        rstd = sb("rstd", [128, TMAX])
        pT = sb("pT", [128, 2, TMAX], BF16)
        S32 = sb("S32", [128, 8, 256])
        Sbf = sb("Sbf", [128, 8, 256], BF16)
        Ss = [sb(f"Ss{i}", [128, 4, 256]) for i in range(4)]
        Qm = sb("Qm", [128, 8, 16, 16])
        Vm = [sb(f"Vm{i}", [16, 1024], BF16) for i in range(2)]
        ks_tok = sb("ks_tok", [16, 1024], BF16)
        sTs = sb("sTs", [16, 4, 16], BF16)
        junk = sg
        on4 = sb("on4", [128, 4, 256], BF16)
        st4 = sb("st4", [128, 8, 4])
        sT4 = sb("sT4", [128, 4, 128], BF16)
        ng = sb("ng_s", [128, L, 16])
        fg = sb("fg_s", [128, 16])
        cw = sb("cw_s", [128, L, 8, 3])
        gg = sb("gg_s", [128, L, 8])
        dq = sb("dq_s", [128, 4, 128])
        dk = sb("dk_s", [128, 4, 128])
        maskT = sb("mask_s", [128, 128])
        c3 = sg[:, 0:384].rearrange("p (a b) -> p a b", b=128)
        identb = sb("identb", [128, 128], BF16)
        onesb = sb("onesb", [128, 128], BF16)
        zb = sb("zb", [128, 16], BF16)
        eyep = sb("eyep_s", [16, 16])
        eyeg = sb("eyeg_s", [128, 16, 16])
        utail = sb("utail", [128, L, 8, 2])
        epsT = sb("epsT", [128, 1])
        scl = sb("scl", [128, 8, 2, NS])
        ncs_sb = sb("ncs_sb", [128, 8, 2, NS])
        ps = [es.enter_context(nc.psum_tensor(f"ps{i}", [128, 512], F32)) for i in range(8)]
        psb = [ps[i][:, :].bitcast(BF16) for i in range(8)]
        BK = {i: [("ps", i)] for i in range(8)}
        B4 = BK[4]
        B7 = BK[7]
        bank_ctr = [0]

        def next_bank():
            b = bank_ctr[0] % 4
            bank_ctr[0] += 1
            return b

        for (dst, src, key) in [(ng, ng_d, "ng"), (fg, fg_d, "fg"), (cw, cw_d, "cw"), (gg, gg_d, "gg"),
                                (dq, dq_d, "dtab"), (dk, dk_d, "dtab2"), (maskT, mask_d, "mask"),
                                (c3, c3_d, "sg"), (eyep, eyep_d, "eyep"), (eyeg, eyeg_d, "eyeg")]:
            P.dma("sp", (lambda e, d=dst, s=src: e.dma_start(out=(d if key == "sg" else d[:]), in_=s)), W=[key])
        P.op("dve", lambda e: e.tensor_copy(out=identb[:, :], in_=c3[:, 0, :]), R=["sg"], W=["identb"])
        P.op("dve", lambda e: e.tensor_copy(out=onesb[:, :], in_=c3[:, 1, :]), R=["sg"], W=["onesb"])
        P.op("dve", lambda e: e.memset(zb[:, :], 0.0), W=["zb"])
        P.op("dve", lambda e: e.memset(utail[:], 0.0), W=["utail"])
        P.op("dve", lambda e: e.memset(epsT[:], 1e-6), W=["epsT"])

        def layer_specs(l):
            wi = w_in[l].rearrange("(kc p) c -> p kc c", p=128)
            wo = w_out[l].rearrange("(kc p) c -> p kc c", p=128)
            wg = w_pg[l].rearrange("(kc p) c -> p kc c", p=128)
            wp = w_ple[l].rearrange("(kc p) c -> p kc c", p=128)
            sp = []
            for base in (4096, 5120, 6144, 7168):
                for h in range(4):
                    sp.append([(wi[:, :, base + 256 * h: base + 256 * h + 256], 0, 16, 256)])
            for j in range(8):
                sp.append([(wi[:, :, j * 128:(j + 1) * 128], 0, 16, 128),
                           (wi[:, :, 1024 + j * 128:1024 + (j + 1) * 128], 128, 16, 128)])
                sp.append([(wi[:, :, 2048 + j * 128:2048 + (j + 1) * 128], 0, 16, 128),
                           (wi[:, :, 3072 + j * 128:3072 + (j + 1) * 128], 128, 16, 128)])
            for b in range(8):
                sp.append([(wo[:, :, 256 * b:256 * b + 256], 0, 16, 256)])
            for b in range(8):
                sp.append([(wg[:, :, 256 * b:256 * b + 256], 0, 16, 256),
                           (wp[:, :, 256 * b:256 * b + 256], -1, 2, 256)])
            return sp

        NBL = len(layer_specs(0))
        all_specs = []
        for p in range(NPASS):
            for l in range(L):
                all_specs.extend(layer_specs(l))
        wcache = [nc.dram_tensor(f"wcache{l}", [NBL, 128, WSLOT], BF16).ap() for l in range(L)]
        wstate = {"issued": 0, "cur": 0}

        def flush_store():
            pend = wstate.get("pend")
            if pend is not None:
                lz, blk, slot, nel, keys = pend
                P.dma("pool", (lambda e: e.dma_start(out=wcache[lz][blk, :, 0:nel], in_=wb[slot][:, 0:nel])),
                      R=keys, W=[("wc", lz, blk)])
                wstate["pend"] = None

        def issue_loads(upto):
            while wstate["issued"] <= min(upto, len(all_specs) - 1):
                i = wstate["issued"]
                slot = i % NWS
                pz = i // (NBL * L)
                lz = (i // NBL) % L
                blk = i % NBL
                spec = all_specs[i]
                nel = 4608 if any(off < 0 for (_, off, _, _) in spec) else 4096
                keys = [("wb", slot, pi) for pi in range(len(spec))]
                cp = CACHE_PLAN[blk % 6]
                can_cache = cp < NPASS - 1
                if pz > cp and can_cache:
                    flush_store()
                    P.dma("pool", (lambda e, lz=lz, blk=blk, slot=slot, nel=nel: e.dma_start(
                        out=wb[slot][:, 0:nel], in_=wcache[lz][blk, :, 0:nel])), R=[("wc", lz, blk)], W=keys)
                else:
                    for pi, (src, off, nk, ncol) in enumerate(spec):
                        if off >= 0:
                            dst = wb[slot][:, 0:4096].rearrange("p (k c) -> p k c", c=256)[:, :, off:off + ncol]
                        else:
                            dst = wb[slot][:, 4096:4608].rearrange("p (k c) -> p k c", c=256)
                        P.dma("pool", (lambda e, d=dst, s=src: e.dma_start(out=d, in_=s)), W=[("wb", slot, pi)])
                    flush_store()
                    if pz == cp and can_cache:
                        wstate["pend"] = (lz, blk, slot, nel, keys)
                wstate["issued"] += 1

        def get_block():
            i = wstate["cur"]
            wstate["cur"] += 1
            issue_loads(i + NWS - 1)
            slot = i % NWS
            keys = [("wb", slot, pi) for pi in range(len(all_specs[i]))]
            wv = wb[slot][:, 0:4096].rearrange("p (k c) -> p k c", c=256)
            wpv = wb[slot][:, 4096:4608].rearrange("p (k c) -> p k c", c=256)
            return wv, wpv, keys

        for p in range(NPASS):
            SP = (NPASS - 1) if SAMPLE_LAST else 0
            T = NPR + (NS if p == SP else 0)
            TCH = [(0, 256), (256, T)] if p == SP else [(0, 512)]
            tok0 = p * NPR
            allx = [("xT", f) for f in range(16)]
            P.dma("sp", (lambda e, t=tok0: e.dma_start(out=xT[:, :, 0:NPR], in_=xp[:, :, t:t + NPR])), W=allx)
            if p == SP:
                P.dma("sp", (lambda e: e.dma_start(out=xT[:, :, NPR:T], in_=xs)), W=allx)
            P.dma("sp", (lambda e, p=p: e.dma_start(out=csT[:], in_=cs_d[:, :, p, :])), W=["cs"])

            def norm_finish(pre_banks, T=T, TCH=TCH):
                for ci, (t0, t1) in enumerate(TCH):
                    bk = pre_banks[ci]
                    n = t1 - t0
                    P.op("act", (lambda e, bk=bk, n=n, t0=t0, t1=t1: e.activation(
                        out=rstd[:, t0:t1], in_=ps[bk][:, 0:n], func=AF.Sqrt, bias=epsT[:, 0:1])),
                        R=[("ps", bk), "epsT"], W=["rstd"])
                    P.op("dve", (lambda e, t0=t0, t1=t1: e.reciprocal(out=rstd[:, t0:t1], in_=rstd[:, t0:t1])),
                         R=["rstd"], W=["rstd"])

            def sq_accum(f, T=T, TCH=TCH):
                P.op("act", (lambda e: e.activation(out=ymT[:, f, 0:T], in_=xT[:, f, 0:T], func=AF.Square)),
                     R=[("xT", f)], W=[("ym", f)])
                sq_pend.append(f)

            def sq_flush(keep, T=T, TCH=TCH):
                while len(sq_pend) > keep:
                    sq_mm(sq_pend.pop(0))

            def sq_mm(f, T=T, TCH=TCH):
                for ci, (t0, t1) in enumerate(TCH):
                    P.op("pe", (lambda e, ci=ci, t0=t0, t1=t1: e.matmul(
                        ps[4 + ci][:, 0:t1 - t0], onesb[:, :], ymT[:, f, t0:t1], start=(f == 0), stop=(f == 15))),
                        R=[("ym", f), "onesb"], W=[("ps", 4 + ci)])

            def norm_stats(T=T, TCH=TCH):
                if state_pre[0]:
                    state_pre[0] = False
                    norm_finish([4, 5])
                    return
                allym = [("ym", f) for f in range(16)]
                P.op("act", lambda e: e.activation(out=ymT[:, :, 0:T], in_=xT[:, :, 0:T], func=AF.Square),
                     R=[("xT", f) for f in range(16)], W=allym)
                for (t0, t1) in TCH:
                    bk = next_bank()
                    n = t1 - t0
                    for kc in range(16):
                        P.op("pe", (lambda e, bk=bk, n=n, kc=kc, t0=t0, t1=t1: e.matmul(
                            ps[bk][:, 0:n], onesb[:, :], ymT[:, kc, t0:t1], start=(kc == 0), stop=(kc == 15))),
                            R=[("ym", kc), "onesb"], W=[("ps", bk)])
                    P.op("act", (lambda e, bk=bk, n=n, t0=t0, t1=t1: e.activation(
                        out=rstd[:, t0:t1], in_=ps[bk][:, 0:n], func=AF.Sqrt, bias=epsT[:, 0:1])),
                        R=[("ps", bk), "epsT"], W=["rstd"])
                    P.op("dve", (lambda e, t0=t0, t1=t1: e.reciprocal(out=rstd[:, t0:t1], in_=rstd[:, t0:t1])),
                         R=["rstd"], W=["rstd"])

            state_pre = [False]
            sq_pend = []
            for l in range(L):
                P.dma("pool", (lambda e, l=l, t=tok0: e.dma_start(out=pT[:, :, 0:NPR], in_=pp[l][:, :, t:t + NPR])),
                      W=["pT"])
                if p == SP:
                    P.dma("pool", (lambda e, l=l: e.dma_start(out=pT[:, :, NPR:T], in_=psm[l])), W=["pT"])
                    P.dma("sp", (lambda e, l=l: e.dma_start(out=scl[:], in_=scT[l])), W=["scl"])
                if p == 0:
                    P.op("dve", lambda e: e.memset(S32[:], 0.0), W=["S32"])
                    P.op("dve", lambda e: e.memset(Sbf[:], 0.0), W=["Sbf"])
                else:
                    P.dma("sp", (lambda e, l=l: e.dma_start(out=S32[:], in_=nrp[l])), R=[("nrp", l)], W=["S32"])
                    P.op("act", lambda e: e.activation(out=Sbf[:], in_=S32[:], func=AF.Copy), R=["S32"], W=["Sbf"])

                norm_stats()
                for kc in range(16):
                    P.op("dve", (lambda e, kc=kc, l=l: e.scalar_tensor_tensor(
                        out=hT[:, kc, 0:T], in0=xT[:, kc, 0:T], scalar=ng[:, l, kc:kc + 1], in1=rstd[:, 0:T],
                        op0=ALU.mult, op1=ALU.mult)), R=[("xT", kc), "rstd", "ng"], W=[("hT", kc)])

                def fm_block(nkc, lhs_fn, rhs_fn, rkeys_fn, wkeys, evac, ncc=2):
                    for cc in range(ncc):
                        for ci, (t0, t1) in enumerate(TCH):
                            bk = next_bank()
                            n = t1 - t0
                            for kc in range(nkc):
                                P.op("pe", (lambda e, bk=bk, n=n, kc=kc, cc=cc, t0=t0, t1=t1: e.matmul(
                                    ps[bk][:, 0:n], lhs_fn(kc, cc), rhs_fn(kc, t0, t1),
                                    start=(kc == 0), stop=(kc == nkc - 1))),
                                    R=wkeys + rkeys_fn(kc), W=[("ps", bk)])
                            evac(cc, ci, t0, t1, bk, n)

                hkeys = lambda kc: [("hT", kc)]
                hrhs = lambda kc, t0, t1: hT[:, kc, t0:t1]

                if CFG["stage"] < 2:
                    continue
                for which, (dstT, tbl, tkey, sscale) in enumerate([(qT, dq, "dtab", 1.0), (kT, dk, "dtab2", 1.0 / 16)]):
                    dname = "q" if which == 0 else "k"
                    for h in range(4):
                        wv, wpv, wk = get_block()

                        def evac_qk(cc, ci, t0, t1, bk, n, h=h, tbl=tbl, tkey=tkey, sscale=sscale):
                            npr = min(t1, NPR) - t0
                            r = npr // 128
                            P.op("dve", (lambda e: e.tensor_tensor(
                                out=raw[:, cc, t0:t0 + npr].rearrange("p (r l) -> p r l", l=128),
                                in0=ps[bk][:, 0:npr].rearrange("p (r l) -> p r l", l=128),
                                in1=tbl[:, h:h + 1, :].broadcast_to([128, r, 128]), op=ALU.mult)),
                                R=[("ps", bk), tkey], W=[("raw", cc)])
                            if t1 > NPR:
                                P.op("dve", (lambda e: e.tensor_scalar_mul(out=raw[:, cc, NPR:t1], in0=ps[bk][:, npr:n],
                                                                           scalar1=sscale)),
                                     R=[("ps", bk)], W=[("raw", cc)])

                        fm_block(16, (lambda kc, cc, wv=wv: wv[:, kc, cc * 128:(cc + 1) * 128]), hrhs, hkeys, wk, evac_qk)
                        x1 = raw[:, 0, 0:T]
                        x2 = raw[:, 1, 0:T]
                        C = csT[:, 0, 0:T]
                        Sn = csT[:, 1, 0:T]
                        tt = [r_[:, 0:T] for r_ in rt]
                        for (o_, a_, b_, ok, ak) in [(tt[0], x1, C, RTK[0], ("raw", 0)), (tt[1], x2, Sn, RTK[1], ("raw", 1)),
                                                     (tt[2], x1, Sn, RTK[2], ("raw", 0)), (tt[3], x2, C, RTK[3], ("raw", 1))]:
                            P.op("dve", (lambda e, o_=o_, a_=a_, b_=b_: e.tensor_tensor(out=o_, in0=a_, in1=b_, op=ALU.mult)),
                                 R=[ak, "cs"], W=[ok])
                        P.op("dve", (lambda e, h=h, dstT=dstT: e.tensor_tensor(out=dstT[:, 2 * h, 0:T], in0=tt[0], in1=tt[1], op=ALU.subtract)),
                             R=[RTK[0], RTK[1]], W=[(dname, 2 * h)])
                        P.op("dve", (lambda e, h=h, dstT=dstT: e.tensor_tensor(out=dstT[:, 2 * h + 1, 0:T], in0=tt[2], in1=tt[3], op=ALU.add)),
                             R=[RTK[2], RTK[3]], W=[(dname, 2 * h + 1)])

                if CFG["stage"] < 3:
                    continue
                tiles = [(i, i * 128, 128) for i in range(4)] + ([(4, NPR, NS)] if p == SP else [])
                for h in range(4):
                    wv, wpv, wk = get_block()
                    for (ti, tk0, ntk) in tiles:
                        bk = next_bank()
                        for kc in range(16):
                            P.op("pe", (lambda e, bk=bk, kc=kc, tk0=tk0, ntk=ntk, wv=wv: e.matmul(
                                ps[bk][0:ntk, 0:256], hT[:, kc, tk0:tk0 + ntk], wv[:, kc, :],
                                start=(kc == 0), stop=(kc == 15))), R=wk + [("hT", kc)], W=[("ps", bk)])
                        P.op("act", (lambda e, bk=bk, ti=ti, ntk=ntk, h=h: e.activation(
                            out=v_tok[0:ntk, ti, h * 256:(h + 1) * 256], in_=ps[bk][0:ntk, 0:256], func=AF.Copy)),
                            R=[("ps", bk)], W=[("v", ti, h)])

                if CFG["stage"] < 4:
                    continue
                for h in range(4):
                    wv, wpv, wk = get_block()

                    def evac_g(cc, ci, t0, t1, bk, n, h=h):
                        f = 8 + 2 * h + cc
                        P.op("act", (lambda e: e.activation(out=ymT[:, f, t0:t1], in_=ps[bk][:, 0:n], func=AF.Silu)),
                             R=[("ps", bk)], W=[("ym", f)])
                        P.op("dve", (lambda e: e.tensor_scalar_mul(out=ymT[:, f, t0:t1], in0=ymT[:, f, t0:t1],
                                                                   scalar1=gg[:, l, f - 8:f - 7])), R=[("ym", f), "gg"], W=[("ym", f)])
                    fm_block(16, (lambda kc, cc, wv=wv: wv[:, kc, cc * 128:(cc + 1) * 128]), hrhs, hkeys, wk, evac_g)

                YG = [("ym", 8 + i_) for i_ in range(8)]

                def post_norm(ntok, psOs, okeys):
                    for h in range(4):
                        P.op("act", (lambda e, h=h: e.activation(out=junk[0:ntok, 0:256], in_=psOs[h], func=AF.Identity,
                                                            accum_out=st4[0:ntok, 0, h:h + 1])), R=okeys[h], W=["sg", "st4"])
                        P.op("act", (lambda e, h=h: e.activation(out=junk[0:ntok, 256:512], in_=psOs[h], func=AF.Square,
                                                            accum_out=st4[0:ntok, 1, h:h + 1])), R=okeys[h], W=["sg", "st4"])
                    s_ = st4
                    P.op("dve", (lambda e: e.tensor_scalar_mul(out=s_[0:ntok, 2:4, :], in0=s_[0:ntok, 0:2, :], scalar1=1.0 / 256)),
                         R=["st4"], W=["st4"])
                    P.op("dve", (lambda e: e.tensor_tensor(out=s_[0:ntok, 4, :], in0=s_[0:ntok, 2, :], in1=s_[0:ntok, 2, :], op=ALU.mult)),
                         R=["st4"], W=["st4"])
                    P.op("dve", (lambda e: e.tensor_tensor(out=s_[0:ntok, 5, :], in0=s_[0:ntok, 3, :], in1=s_[0:ntok, 4, :], op=ALU.subtract)),
                         R=["st4"], W=["st4"])
                    P.op("act", (lambda e: e.activation(out=s_[0:ntok, 6, :], in_=s_[0:ntok, 5, :], func=AF.Sqrt,
                                                        bias=epsT[0:ntok, 0:1])), R=["st4", "epsT"], W=["st4"])
                    P.op("dve", (lambda e: e.reciprocal(out=s_[0:ntok, 6, :], in_=s_[0:ntok, 6, :])), R=["st4"], W=["st4"])
                    P.op("dve", (lambda e: e.scalar_tensor_tensor(out=s_[0:ntok, 7, :], in0=s_[0:ntok, 2, :], scalar=-1.0,
                                                                  in1=s_[0:ntok, 6, :], op0=ALU.mult, op1=ALU.mult)),
                         R=["st4"], W=["st4"])
                    for h in range(4):
                        P.op("act", (lambda e, h=h: e.activation(out=on4[0:ntok, h, :], in_=psOs[h], func=AF.Identity,
                                                            scale=s_[0:ntok, 6, h:h + 1], bias=s_[0:ntok, 7, h:h + 1])),
                             R=okeys[h] + ["st4"], W=["on4"])

                def post_tr(ntok, c0, ob):
                    for half in range(2):
                        for h in range(4):
                            P.op("pe", (lambda e, half=half, h=h: e.transpose(
                                psb[ob][:, half * 512 + h * 128:half * 512 + h * 128 + ntok],
                                on4[0:ntok, h, half * 128:(half + 1) * 128], identb[0:ntok, 0:ntok])),
                                R=["on4", "identb"], W=[("ps", ob)])
                    for half in range(2):
                        yv = ymT[:, 8 + half:16:2, c0:c0 + ntok]
                        pv = psb[ob][:, half * 512:(half + 1) * 512].rearrange("p (h t) -> p h t", t=128)[:, :, 0:ntok]
                        P.op("dve", (lambda e, yv=yv, pv=pv: e.tensor_tensor(out=yv, in0=pv, in1=yv, op=ALU.mult)),
                             R=[("ps", ob)] + YG, W=YG)

                def sample_prep():
                    for hh in range(8):
                        P.op("pe", (lambda e, hh=hh: e.transpose(psb[7][0:NS, hh * 128:(hh + 1) * 128], kT[:, hh, NPR:T], identb[:, :])),
                             R=[("k", hh), "identb"], W=B7)
                    P.op("dve", lambda e: e.tensor_copy(out=ks_tok[:, :], in_=psb[7][0:NS, 0:1024]), R=B7, W=["ks_tok"])
                    for h in range(4):
                        for half in range(2):
                            P.op("pe", (lambda e, h=h, half=half: e.matmul(
                                ps[4][0:NS, h * 16:(h + 1) * 16], kT[:, 2 * h + half, NPR:T], qT[:, 2 * h + half, NPR:T],
                                start=(half == 0), stop=(half == 1))), R=[("k", 2 * h + half), ("q", 2 * h + half)], W=B4)
                    P.op("dve", lambda e: e.tensor_tensor(
                        out=sTs[:, :, :], in0=ps[4][0:NS, 0:64].rearrange("p (h t) -> p h t", t=16),
                        in1=eyep[:, :].unsqueeze(1).broadcast_to([NS, 4, 16]), op=ALU.mult), R=B4 + ["eyep"], W=["sTs"])
                    for hh in range(8):
                        P.op("dve", (lambda e, hh=hh: e.scalar_tensor_tensor(
                            out=Qm[:, hh, :, :], in0=qT[:, hh, NPR:T].unsqueeze(2).broadcast_to([128, 16, 16]),
                            scalar=GAM[hh // 2], in1=eyeg[:, :, :], op0=ALU.mult, op1=ALU.mult)),
                            R=[("q", hh), "eyeg"], W=[("Qm", hh)])
                    for bkk in (4, 5):
                        P.op("pe", (lambda e, bkk=bkk: e.matmul(ps[bkk][0:NS, 0:512], zb[:, 0:NS], hT[:, 0, 0:512],
                                                                start=True, stop=False, skip_group_check=True)),
                             R=["zb", ("hT", 0)], W=BK[bkk])
                    for h in range(4):
                        P.op("pe", (lambda e, h=h: e.matmul(
                            ps[4 + h // 2][0:NS, (h % 2) * 256:(h % 2) * 256 + 256], sTs[:, h, :],
                            v_tok[0:NS, 4, h * 256:(h + 1) * 256], start=False, stop=False, skip_group_check=True)),
                            R=["sTs"] + [("v", 4, h)], W=BK[4 + h // 2])

                def job_load(jb):
                    b, hp = jb // 2, jb % 2
                    sl = jb % 4
                    src = sret[l, b, 2 * hp:2 * hp + 2].rearrange("h (hf p) e -> p (h hf) e", p=128)
                    P.dma("sp", (lambda e: e.dma_start(out=Ss[sl][:], in_=src)), W=[("Ss", sl)])

                def sample_token(b):
                    vs = b % 2
                    P.op("dve", (lambda e: e.tensor_scalar_mul(out=Vm[vs][:, :], in0=v_tok[0:NS, 4, :], scalar1=eyep[:, b:b + 1])),
                         R=[("v", 4, h_) for h_ in range(4)] + ["eyep"], W=[("Vm", vs)])
                    for hp in range(2):
                        jb = 2 * b + hp
                        sl = jb % 4
                        if jb + 2 < 2 * NS:
                            job_load(jb + 2)
                        for q4 in range(4):
                            hh = 4 * hp + q4
                            h = hh // 2
                            P.op("pe", (lambda e, hh=hh, h=h, q4=q4: e.matmul(
                                ps[4 + hp][0:NS, (h % 2) * 256:(h % 2) * 256 + 256], Qm[:, hh, b, :], Ss[sl][:, q4, :],
                                start=False, stop=False, skip_group_check=True)),
                                R=[("Qm", hh), ("Ss", sl)], W=BK[4 + hp])
                        for hi in range(2):
                            h = 2 * hp + hi
                            bk = 6 + hi
                            for half in range(2):
                                P.op("pe", (lambda e, h=h, half=half, bk=bk: e.matmul(
                                    ps[bk][:, half * 256:(half + 1) * 256],
                                    ks_tok[:, h * 256 + half * 128:h * 256 + (half + 1) * 128],
                                    Vm[vs][:, h * 256:(h + 1) * 256], start=True, stop=True)),
                                    R=["ks_tok", ("Vm", vs)], W=BK[bk])
                            P.op("dve", (lambda e, h=h, hi=hi, bk=bk: e.scalar_tensor_tensor(
                                out=Ss[sl][:, 2 * hi:2 * hi + 2, :], in0=Ss[sl][:, 2 * hi:2 * hi + 2, :], scalar=GAM[h],
                                in1=ps[bk][:, :].rearrange("p (a e) -> p a e", e=256), op0=ALU.mult, op1=ALU.add)),
                                R=BK[bk] + [("Ss", sl)], W=[("Ss", sl)])
                        dst = nrs[l, b, 2 * hp:2 * hp + 2].rearrange("h (hf p) e -> p (h hf) e", p=128)
                        P.dma("sp", (lambda e, dst=dst: e.dma_start(out=dst, in_=Ss[sl][:])), R=[("Ss", sl)])

                def ret_R1(c):
                    tk = slice(c * 128, (c + 1) * 128)
                    kslot = c % 2
                    for h in range(4):
                        for half in range(2):
                            P.op("pe", (lambda e, h=h, half=half: e.transpose(
                                psb[4][:, h * 256 + half * 128:h * 256 + (half + 1) * 128], kT[:, 2 * h + half, tk], identb[:, :])),
                                R=[("k", 2 * h + half), "identb"], W=[("ps", 4)])
                    for h in range(4):
                        P.op("act", (lambda e, h=h: e.activation(
                            out=k_tok[kslot][:, h * 256:(h + 1) * 256], in_=psb[4][:, h * 256:(h + 1) * 256], func=AF.Copy, scale=G128[h])),
                            R=[("ps", 4)], W=[("ktok", kslot)])
                    for h in range(4):
                        for half in range(2):
                            P.op("pe", (lambda e, h=h, half=half: e.matmul(
                                ps[5][:, h * 128:(h + 1) * 128], kT[:, 2 * h + half, tk], qT[:, 2 * h + half, tk],
                                start=(half == 0), stop=(half == 1))),
                                R=[("k", 2 * h + half), ("q", 2 * h + half)], W=[("ps", 5)])
                    P.op("dve", (lambda e: e.tensor_tensor(
                        out=sT4[:, :, :], in0=ps[5][:, :].rearrange("p (h t) -> p h t", t=128),
                        in1=maskT[:, :].unsqueeze(1).broadcast_to([128, 4, 128]), op=ALU.mult)),
                        R=[("ps", 5), "mask"], W=["sT4"])

                def ret_R2(c):
                    tk = slice(c * 128, (c + 1) * 128)
                    psOs = []
                    for h in range(4):
                        ob = 6 + h // 2
                        reg = (h % 2) * 256
                        psOs.append(ps[ob][:, reg:reg + 256])
                        P.op("pe", (lambda e, h=h, ob=ob, reg=reg: e.matmul(
                            ps[ob][:, reg:reg + 256], sT4[:, h, :], v_tok[:, c, h * 256:(h + 1) * 256], start=True, stop=False)),
                            R=["sT4", ("v", c, h)], W=[("ps", ob)])
                        for half in range(2):
                            P.op("pe", (lambda e, h=h, half=half, ob=ob, reg=reg: e.matmul(
                                ps[ob][:, reg:reg + 256], qT[:, 2 * h + half, tk], Sbf[:, 2 * h + half, :],
                                start=False, stop=(half == 1))), R=[("q", 2 * h + half), "Sbf"], W=[("ps", ob)])
                    post_norm(128, psOs, [BK[6 + h // 2] for h in range(4)])

                def ret_KV(c, heads, banks):
                    kslot = c % 2
                    for h, bk in zip(heads, banks):
                        for half in range(2):
                            P.op("pe", (lambda e, h=h, half=half, bk=bk: e.matmul(
                                ps[bk][:, half * 256:(half + 1) * 256],
                                k_tok[kslot][:, h * 256 + half * 128:h * 256 + (half + 1) * 128],
                                v_tok[:, c, h * 256:(h + 1) * 256], start=True, stop=True)),
                                R=[("ktok", kslot), ("v", c, h)], W=[("ps", bk)])
                    for h, bk in zip(heads, banks):
                        P.op("dve", (lambda e, h=h, bk=bk: e.scalar_tensor_tensor(
                            out=S32[:, 2 * h:2 * h + 2, :], in0=S32[:, 2 * h:2 * h + 2, :], scalar=G128[h],
                            in1=ps[bk][:, :].rearrange("p (a e) -> p a e", e=256), op0=ALU.mult, op1=ALU.add)),
                            R=[("ps", bk), "S32"], W=["S32"])
                        P.op("act", (lambda e, h=h: e.activation(out=Sbf[:, 2 * h:2 * h + 2, :], in_=S32[:, 2 * h:2 * h + 2, :],
                                                                 func=AF.Copy)), R=["S32"], W=["Sbf"])

                ret_stages = []
                if p != SP and CFG["stage"] >= 7:
                    for c in range(4):
                        if c == 0:
                            ret_stages.append([lambda c=c: ret_R1(c)])
                        ret_stages.append([lambda c=c: ret_R2(c)])
                        ret_stages.append([lambda c=c: post_tr(128, c * 128, 4), lambda c=c: ret_KV(c, [0, 1], [6, 7])])
                        last = [lambda c=c: ret_KV(c, [2, 3], [6, 7])]
                        if c < 3:
                            last.append(lambda c=c: ret_R1(c + 1))
                        ret_stages.append(last)

                def run_stage():
                    if ret_stages:
                        for fn_ in ret_stages.pop(0):
                            fn_()

                if CFG["stage"] < 5:
                    continue
                for j in range(8):
                    wv, wpv, wk = get_block()

                    def evac_a(cc, ci, t0, t1, bk, n):
                        dst = tb if cc == 0 else tc
                        P.op("act", (lambda e: e.activation(out=dst[:, t0:t1], in_=ps[bk][:, 0:n], func=AF.Copy)),
                             R=[("ps", bk)], W=["tb" if cc == 0 else "tc"])
                    fm_block(16, (lambda kc, cc, wv=wv: wv[:, kc, cc * 128:(cc + 1) * 128]), hrhs, hkeys, wk, evac_a)
                    run_stage()
                    wv, wpv, wk = get_block()

                    def evac_b(cc, ci, t0, t1, bk, n, j=j):
                        if cc == 0:
                            P.op("dve", (lambda e: e.tensor_tensor(out=uext[:, 2 + t0:2 + t1], in0=ps[bk][:, 0:n],
                                                                   in1=tc[:, t0:t1], op=ALU.mult)),
                                 R=[("ps", bk), "tc"], W=["u"])
                        else:
                            P.op("act", (lambda e: e.activation(out=ymT[:, j, t0:t1], in_=ps[bk][:, 0:n], func=AF.Silu)),
                                 R=[("ps", bk)], W=[("ym", j)])
                    fm_block(16, (lambda kc, cc, wv=wv: wv[:, kc, cc * 128:(cc + 1) * 128]), hrhs, hkeys, wk, evac_b)
                    w0 = cw[:, l, j, 0:1]
                    w1 = cw[:, l, j, 1:2]
                    w2 = cw[:, l, j, 2:3]
                    P.op("dve", (lambda e, j=j: e.tensor_copy(out=uext[:, 0:2], in_=utail[:, l, j, :])), R=["utail"], W=["u"])
                    P.op("dve", (lambda e, w0=w0: e.tensor_scalar_mul(out=acc[:, 0:NPR], in0=uext[:, 0:NPR], scalar1=w0)), R=["u", "cw"], W=["acc"])
                    P.op("dve", (lambda e, w1=w1: e.scalar_tensor_tensor(out=acc[:, 0:NPR], in0=uext[:, 1:NPR + 1], scalar=w1,
                                                                         in1=acc[:, 0:NPR], op0=ALU.mult, op1=ALU.add)),
                         R=["u", "acc"], W=["acc"])
                    P.op("dve", (lambda e, w2=w2: e.scalar_tensor_tensor(out=acc[:, 0:NPR], in0=uext[:, 2:NPR + 2], scalar=w2,
                                                                         in1=acc[:, 0:NPR], op0=ALU.mult, op1=ALU.add)),
                         R=["u", "acc"], W=["acc"])
                    P.op("dve", (lambda e: e.tensor_tensor(out=acc[:, 0:NPR], in0=acc[:, 0:NPR], in1=tb[:, 0:NPR], op=ALU.mult)),
                         R=["acc", "tb"], W=["acc"])
                    P.op("dve", (lambda e, j=j: e.tensor_tensor(out=ymT[:, j, 0:NPR], in0=acc[:, 0:NPR], in1=ymT[:, j, 0:NPR],
                                                                op=ALU.mult)), R=["acc", ("ym", j)], W=[("ym", j)])
                    P.op("dve", (lambda e, j=j: e.tensor_copy(out=utail[:, l, j, :], in_=uext[:, NPR:NPR + 2])), R=["u"], W=["utail"])
                    run_stage()
                    if p == SP:
                        us = uext[:, 2 + NPR:2 + T]
                        sa = acc[:, NPR:T]
                        P.op("dve", (lambda e, j=j, w0=w0: e.tensor_scalar_mul(out=sa, in0=scl[:, j, 0, :], scalar1=w0)), R=["scl", "cw", "acc"], W=["acc"])
                        P.op("dve", (lambda e, j=j, w1=w1: e.scalar_tensor_tensor(out=sa, in0=scl[:, j, 1, :], scalar=w1, in1=sa,
                                                                                  op0=ALU.mult, op1=ALU.add)), R=["scl", "acc"], W=["acc"])
                        P.op("dve", (lambda e, w2=w2: e.scalar_tensor_tensor(out=sa, in0=us, scalar=w2, in1=sa,
                                                                             op0=ALU.mult, op1=ALU.add)), R=["u", "acc"], W=["acc"])
                        P.op("dve", (lambda e: e.tensor_tensor(out=sa, in0=sa, in1=tb[:, NPR:T], op=ALU.mult)), R=["acc", "tb"], W=["acc"])
                        P.op("dve", (lambda e, j=j: e.tensor_tensor(out=ymT[:, j, NPR:T], in0=sa, in1=ymT[:, j, NPR:T], op=ALU.mult)),
                             R=["acc", ("ym", j)], W=[("ym", j)])
                        P.op("dve", (lambda e, j=j: e.tensor_copy(out=ncs_sb[:, j, 0, :], in_=scl[:, j, 1, :])), R=["scl"], W=["ncs_sb"])
                        P.op("dve", (lambda e, j=j: e.tensor_copy(out=ncs_sb[:, j, 1, :], in_=us)), R=["u"], W=["ncs_sb"])
                        if j == 0:
                            sample_prep()
                            job_load(0)
                            job_load(1)
                        sample_token(2 * j)
                        sample_token(2 * j + 1)
                if p == SP:
                    P.dma("sp", (lambda e, l=l: e.dma_start(out=ncs[l], in_=ncs_sb[:])), R=["ncs_sb"])
                    post_norm(NS, [ps[4 + h // 2][0:NS, (h % 2) * 256:(h % 2) * 256 + 256] for h in range(4)],
                              [BK[4 + h // 2] for h in range(4)])
                    post_tr(NS, NPR, 6)
                if p == NPASS - 1:
                    P.dma("sp", (lambda e, l=l: e.dma_start(out=ncp[l], in_=utail[:, l, :, :])), R=["utail"])

                if CFG["stage"] < 7:
                    continue

                if p == SP:
                    for c in range(4):
                        ret_R1(c)
                        ret_R2(c)
                        post_tr(128, c * 128, 4)
                        ret_KV(c, [0, 1], [6, 7])
                        ret_KV(c, [2, 3], [0, 1])
                else:
                    while ret_stages:
                        run_stage()
                P.dma("sp", (lambda e, l=l: e.dma_start(out=nrp[l], in_=S32[:])), R=["S32"], W=[("nrp", l)])

                if CFG["stage"] < 8:
                    continue
                ykeys = lambda kc: [("ym", kc)]
                yrhs = lambda kc, t0, t1: ymT[:, kc, t0:t1]
                for b in range(8):
                    wv, wpv, wk = get_block()

                    def evac_o(cc, ci, t0, t1, bk, n, b=b):
                        f = 2 * b + cc
                        P.op("dve", (lambda e: e.tensor_tensor(out=xT[:, f, t0:t1], in0=ps[bk][:, 0:n], in1=xT[:, f, t0:t1], op=ALU.add)),
                             R=[("ps", bk), ("xT", f)], W=[("xT", f)])
                    fm_block(16, (lambda kc, cc, wv=wv: wv[:, kc, cc * 128:(cc + 1) * 128]), yrhs, ykeys, wk, evac_o)
                    for cc in range(2):
                        f = 2 * b + cc
                        P.op("act", (lambda e, f=f: e.activation(out=hT[:, f, 0:T], in_=xT[:, f, 0:T], func=AF.Copy)),
                             R=[("xT", f)], W=[("hT", f)])

                if CFG["stage"] < 9:
                    continue
                for b in range(8):
                    wv, wpv, wk = get_block()
                    for cc in range(2):
                        f = 2 * b + cc
                        for (t0, t1) in TCH:
                            n = t1 - t0
                            bp = next_bank()
                            for k2 in range(2):
                                P.op("pe", (lambda e, bp=bp, n=n, k2=k2, cc=cc, t0=t0, t1=t1, wpv=wpv: e.matmul(
                                    ps[bp][:, 0:n], wpv[:, k2, cc * 128:(cc + 1) * 128], pT[:, k2, t0:t1],
                                    start=(k2 == 0), stop=(k2 == 1))), R=wk + ["pT"], W=[("ps", bp)])
                            bg = next_bank()
                            for kc in range(16):
                                P.op("pe", (lambda e, bg=bg, n=n, kc=kc, cc=cc, t0=t0, t1=t1, wv=wv: e.matmul(
                                    ps[bg][:, 0:n], wv[:, kc, cc * 128:(cc + 1) * 128], hT[:, kc, t0:t1],
                                    start=(kc == 0), stop=(kc == 15))), R=wk + [("hT", kc)], W=[("ps", bg)])
                            P.op("act", (lambda e, bg=bg, n=n, t0=t0, t1=t1: e.activation(out=sg[:, t0:t1], in_=ps[bg][:, 0:n], func=AF.Sigmoid)),
                                 R=[("ps", bg)], W=["sg"])
                            P.op("dve", (lambda e, bp=bp, n=n, t0=t0, t1=t1: e.tensor_tensor(out=sg[:, t0:t1], in0=ps[bp][:, 0:n],
                                                                                             in1=sg[:, t0:t1], op=ALU.mult)),
                                 R=[("ps", bp), "sg"], W=["sg"])
                            P.op("dve", (lambda e, f=f, t0=t0, t1=t1: e.tensor_tensor(out=xT[:, f, t0:t1], in0=xT[:, f, t0:t1],
                                                                                      in1=sg[:, t0:t1], op=ALU.add)),
                                 R=["sg", ("xT", f)], W=[("xT", f)])
                        if PRE_NORM:
                            sq_flush(1)
                            sq_accum(f)
                if PRE_NORM and CFG["stage"] >= 9:
                    sq_flush(0)
                    state_pre[0] = True

            norm_stats()
            for kc in range(16):
                P.op("dve", (lambda e, kc=kc: e.scalar_tensor_tensor(
                    out=xT[:, kc, 0:T], in0=xT[:, kc, 0:T], scalar=fg[:, kc:kc + 1], in1=rstd[:, 0:T],
                    op0=ALU.mult, op1=ALU.mult)), R=[("xT", kc), "rstd", "fg"], W=[("xT", kc)])
            P.dma("sp", (lambda e, t=tok0: e.dma_start(out=yT[:, :, t:t + NPR], in_=xT[:, :, 0:NPR])), R=allx)
            if p == SP:
                P.dma("sp", (lambda e: e.dma_start(out=ysT, in_=xT[:, :, NPR:T])), R=allx)

        sems = {}
        for ename in ["pe", "act", "dve", "pool"]:
            sems[ename] = es.enter_context(nc.semaphore("s_" + ename))
        for q, n in P.nslot.items():
            for i in range(n):
                sems[("dma", q, i)] = es.enter_context(nc.semaphore(f"d_{q}_{i}"))
        for q, n in P.nslot.items():
            for i in range(n):
                if P.slot_uses[q][i] > 0:
                    P.lists["sp"].append(("wait", ("dma", q, i), 16 * P.slot_uses[q][i]))
        for ename in ["pe", "act", "dve"]:
            P.lists["sp"].append(("wait", ename, P.count[ename]))

        block = es.enter_context(nc.Block())

        def replay(name, e):
            for it in P.lists[name]:
                if it[0] == "wait":
                    e.wait_ge(sems[it[1]], it[2])
                elif it[0] == "op":
                    nm, a, k = it[1]
                    getattr(e, nm)(*a, **k).then_inc(sems[name], 1)
                else:
                    nm, a, k = it[1]
                    getattr(e, nm)(*a, **k).then_inc(sems[it[2]], 16)

        @block.tensor
        def _(e):
            replay("pe", e)

        @block.scalar
        def _(e):
            replay("act", e)

        @block.vector
        def _(e):
            replay("dve", e)

        @block.gpsimd
        def _(e):
            replay("pool", e)

        @block.sync
        def _(e):
            replay("sp", e)
    return nc


def _tables():
    half = 128
    inv = np.power(np.float32(10000.0), -(np.arange(half, dtype=np.float32) / np.float32(half))).astype(np.float32)
    cs = np.zeros((128, 2, NPASS, TMAX), np.float32)
    for p in range(NPASS):
        pos = np.concatenate([np.arange(p * NPR, (p + 1) * NPR), np.full(NS, 16384)]).astype(np.float32)
        ang = (pos[None, :] * inv[:, None]).astype(np.float32).astype(np.float64)
        cs[:, 0, p, :] = np.cos(ang)
        cs[:, 1, p, :] = np.sin(ang)
    idx = np.arange(128, dtype=np.float64)
    dq = np.zeros((128, 4, 128), np.float32)
    dk = np.zeros((128, 4, 128), np.float32)
    eyeg = np.zeros((128, 16, 16), np.float32)
    eyeg[:] = np.eye(16)[None]
    for h in range(4):
        dq[:, h, :] = (GAM[h] ** (idx + 1.0))[None, :]
        dk[:, h, :] = (GAM[h] ** (-(idx + 1.0)) / 16.0)[None, :]
    m = np.arange(128)
    maskT = (m[:, None] <= m[None, :]).astype(np.float32)
    c3 = np.zeros((128, 3, 128), np.float32)
    c3[:, 0, :] = np.eye(128)
    c3[:, 1, :] = 1.0 / 2048.0
    return dict(cs=cs, dq=dq, dk=dk, eyeg=eyeg, maskT=maskT, c3=c3, eyep=np.eye(16, dtype=np.float32))


def _fm(a):
    tok, d = a.shape
    return np.ascontiguousarray(a.reshape(tok, d // 128, 128).transpose(2, 1, 0))


def kernel(x_prompt, x_sample, state_conv, state_ret, p_prompt, p_sample,
           norm_g, w_in, conv_w, gn_g, w_out, w_pg, w_ple, final_g):
    f = lambda a: np.ascontiguousarray(np.asarray(a, dtype=np.float32))
    x_prompt, x_sample, state_conv, state_ret = f(x_prompt), f(x_sample), f(state_conv), f(state_ret)
    p_prompt, p_sample, norm_g, w_in, conv_w = f(p_prompt), f(p_sample), f(norm_g), f(w_in), f(conv_w)
    gn_g, w_out, w_pg, w_ple, final_g = f(gn_g), f(w_out), f(w_pg), f(w_ple), f(final_g)
    tabs = _tables()
    shared = dict(w_in=w_in, w_out=w_out, w_pg=w_pg, w_ple=w_ple,
                  ng=np.ascontiguousarray(norm_g.reshape(L, 16, 128).transpose(2, 0, 1)),
                  fg=np.ascontiguousarray(final_g.reshape(16, 128).T),
                  cw=np.ascontiguousarray(conv_w.reshape(L, 3, 8, 128).transpose(3, 0, 2, 1)),
                  gg=np.ascontiguousarray(gn_g.reshape(L, 8, 128).transpose(2, 0, 1)), **tabs)
    in_maps = []
    for c in range(NCORE):
        b = PMAP.get(c)
        s0 = c * NS
        m = dict(shared)
        if b is None:
            m["xp"] = np.zeros((128, 16, SEQ), np.float32)
            m["pp"] = np.zeros((L, 128, 2, SEQ), np.float32)
        else:
            m["xp"] = _fm(x_prompt[b])
            m["pp"] = np.stack([_fm(p_prompt[l, b]) for l in range(L)])
        m["xs"] = _fm(x_sample[s0:s0 + NS, 0, :])
        m["psm"] = np.stack([_fm(p_sample[l, s0:s0 + NS, 0, :]) for l in range(L)])
        sc = state_conv[:, s0:s0 + NS].reshape(L, NS, 2, 8, 128)
        m["scT"] = np.ascontiguousarray(sc.transpose(0, 4, 3, 2, 1))
        m["sret"] = np.ascontiguousarray(state_ret[:, s0:s0 + NS])
        in_maps.append(m)
    nc = build_nc()
    res = run_bass_kernel_spmd(nc, in_maps, core_ids=list(range(NCORE)))
    R = res.results
    y_prompt = np.zeros((4, SEQ, D), np.float32)
    y_sample = np.zeros((128, 1, D), np.float32)
    ncp_o = np.zeros((L, 4, 2, 1024), np.float32)
    nrp_o = np.zeros((L, 4, 4, 256, 256), np.float32)
    ncs_o = np.zeros((L, 128, 2, 1024), np.float32)
    nrs_o = np.zeros((L, 128, 4, 256, 256), np.float32)
    for c in range(NCORE):
        r = R[c]
        s0 = c * NS
        y_sample[s0:s0 + NS, 0, :] = np.asarray(r["ysT"]).transpose(2, 1, 0).reshape(NS, D)
        ncs_o[:, s0:s0 + NS] = np.asarray(r["ncs"]).transpose(0, 4, 3, 2, 1).reshape(L, NS, 2, 1024)
        nrs_o[:, s0:s0 + NS] = np.asarray(r["nrs"])
        c_b = PMAP.get(c)
        if c_b is not None:
            y_prompt[c_b] = np.asarray(r["yT"]).transpose(2, 1, 0).reshape(SEQ, D)
            ncp_o[:, c_b] = np.asarray(r["ncp"]).transpose(0, 3, 2, 1).reshape(L, 2, 1024)
            nrp_o[:, c_b] = np.asarray(r["nrp"]).reshape(L, 128, 4, 2, 256).transpose(0, 2, 3, 1, 4).reshape(L, 4, 256, 256)
    return (y_prompt, y_sample, ncp_o, nrp_o, ncs_o, nrs_o)
```

```python
import numpy as np
from contextlib import ExitStack
import concourse.bass as bass
import concourse.mybir as mybir
from concourse.bass_utils import run_bass_kernel_spmd

F32 = mybir.dt.float32
BF16 = mybir.dt.bfloat16
ALU = mybir.AluOpType
AF = mybir.ActivationFunctionType

D = 2048
L = 4
SEQ = 2048
NPR = 512
NS = 16
NPASS = 4
TMAX = NPR + NS
NCORE = 8
GAM = [1.0 - 2.0 ** (-5 - h) for h in range(4)]
G128 = [g ** 128 for g in GAM]
WSLOT = 16 * 256 + 2 * 256
NWS = 3
CFG = {"stage": 99}
CACHE_PLAN = [0, 0, 0, 1, 2, 2]
SAMPLE_LAST = True
PRE_NORM = True
PMAP = {0: 0, 1: 1, 4: 2, 5: 3}


class _Rec:
    def __getattr__(self, name):
        def f(*a, **k):
            return (name, a, k)
        return f


_REC = _Rec()


class Plan:
    def __init__(self):
        self.engs = ["pe", "act", "dve", "pool", "sp"]
        self.lists = {e: [] for e in self.engs}
        self.count = {e: 0 for e in self.engs}
        self.seen = {e: {} for e in self.engs}
        self.lastw = {}
        self.readers = {}
        self.nslot = {"sp": 20, "pool": 12}
        self.slot_uses = {q: [0] * n for q, n in self.nslot.items()}
        self.slot_next = {q: 0 for q in self.nslot}

    def _wait(self, eng, st):
        if st is None:
            return
        key, val = st
        if eng == "pe" and key == "pe":
            return
        if self.seen[eng].get(key, 0) >= val:
            return
        self.seen[eng][key] = val
        self.lists[eng].append(("wait", key, val))

    def _deps(self, eng, R, W):
        for r in R:
            self._wait(eng, self.lastw.get(r))
        for w in W:
            self._wait(eng, self.lastw.get(w))
            for st in self.readers.get(w, ()):
                self._wait(eng, st)

    def _record(self, stamp, R, W):
        for r in R:
            self.readers.setdefault(r, []).append(stamp)
        for w in W:
            self.lastw[w] = stamp
            self.readers[w] = []

    def op(self, eng, fn, R=(), W=()):
        W = list(W) + [r for r in R if isinstance(r, tuple) and r[0] == "ps"]
        self._deps(eng, R, W)
        self.count[eng] += 1
        stamp = (eng, self.count[eng])
        self.lists[eng].append(("op", fn(_REC)))
        self._record(stamp, R, W)

    def dma(self, q, fn, R=(), W=()):
        i = self.slot_next[q]
        self.slot_next[q] = (i + 1) % self.nslot[q]
        uses = self.slot_uses[q][i]
        key = ("dma", q, i)
        if uses > 0:
            self._wait(q, (key, 16 * uses))
        self._deps(q, R, W)
        self.slot_uses[q][i] += 1
        stamp = (key, 16 * (uses + 1))
        self.lists[q].append(("dma", fn(_REC), key))
        self._record(stamp, R, W)


def build_nc():
    nc = bass.Bass("TRN2", target_bir_lowering=False)

    def din(name, shape):
        return nc.dram_tensor(name, list(shape), F32, kind="ExternalInput").ap()

    def dout(name, shape):
        return nc.dram_tensor(name, list(shape), F32, kind="ExternalOutput").ap()

    xp = din("xp", [128, 16, SEQ])
    xs = din("xs", [128, 16, NS])
    pp = din("pp", [L, 128, 2, SEQ])
    psm = din("psm", [L, 128, 2, NS])
    scT = din("scT", [L, 128, 8, 2, NS])
    sret = din("sret", [L, NS, 4, 256, 256])
    w_in = din("w_in", [L, D, 8192])
    w_out = din("w_out", [L, D, D])
    w_pg = din("w_pg", [L, D, D])
    w_ple = din("w_ple", [L, 256, D])
    ng_d = din("ng", [128, L, 16])
    fg_d = din("fg", [128, 16])
    cw_d = din("cw", [128, L, 8, 3])
    gg_d = din("gg", [128, L, 8])
    cs_d = din("cs", [128, 2, NPASS, TMAX])
    dq_d = din("dq", [128, 4, 128])
    dk_d = din("dk", [128, 4, 128])
    mask_d = din("maskT", [128, 128])
    c3_d = din("c3", [128, 3, 128])
    eyep_d = din("eyep", [16, 16])
    eyeg_d = din("eyeg", [128, 16, 16])

    yT = dout("yT", [128, 16, SEQ])
    ysT = dout("ysT", [128, 16, NS])
    ncp = dout("ncp", [L, 128, 8, 2])
    nrp = dout("nrp", [L, 128, 8, 256])
    ncs = dout("ncs", [L, 128, 8, 2, NS])
    nrs = dout("nrs", [L, NS, 4, 256, 256])

    P = Plan()
    with ExitStack() as es:
        def sb(name, shape, dt=F32):
            return es.enter_context(nc.sbuf_tensor(name, list(shape), dt))

        xT = sb("xT", [128, 16, TMAX])
        hT = sb("hT", [128, 16, TMAX], BF16)
        ymT = sb("ymT", [128, 16, TMAX], BF16)
        wb = [sb(f"wb{i}", [128, WSLOT], BF16) for i in range(NWS)]
        qT = sb("qT", [128, 8, TMAX], BF16)
        kT = sb("kT", [128, 8, TMAX], BF16)
        v_tok = sb("v_tok", [128, 5, 1024], BF16)
        k_tok = [sb(f"k_tok{i}", [128, 1024], BF16) for i in range(2)]
        csT = sb("csT", [128, 2, TMAX])
        raw = sb("raw", [128, 2, TMAX])
        tb = sb("tb", [128, TMAX])
        tc = sb("tc", [128, TMAX])
        uext = sb("uext", [128, TMAX + 2])
        acc = sb("acc", [128, TMAX])
        sg = sb("sg", [128, TMAX])
        rt = [tb, tc, acc, sg]
        RTK = ["tb", "tc", "acc", "sg"]
        rstd = sb("rstd", [128, TMAX])
        pT = sb("pT", [128, 2, TMAX], BF16)
        S32 = sb("S32", [128, 8, 256])
        Sbf = sb("Sbf", [128, 8, 256], BF16)
        Ss = [sb(f"Ss{i}", [128, 4, 256]) for i in range(4)]
        Qm = sb("Qm", [128, 8, 16, 16])
        Vm = [sb(f"Vm{i}", [16, 1024], BF16) for i in range(2)]
        ks_tok = sb("ks_tok", [16, 1024], BF16)
        sTs = sb("sTs", [16, 4, 16], BF16)
        junk = sg
        on4 = sb("on4", [128, 4, 256], BF16)
        st4 = sb("st4", [128, 8, 4])
        sT4 = sb("sT4", [128, 4, 128], BF16)
        ng = sb("ng_s", [128, L, 16])
        fg = sb("fg_s", [128, 16])
        cw = sb("cw_s", [128, L, 8, 3])
        gg = sb("gg_s", [128, L, 8])
        dq = sb("dq_s", [128, 4, 128])
        dk = sb("dk_s", [128, 4, 128])
        maskT = sb("mask_s", [128, 128])
        c3 = sg[:, 0:384].rearrange("p (a b) -> p a b", b=128)
        identb = sb("identb", [128, 128], BF16)
        onesb = sb("onesb", [128, 128], BF16)
        zb = sb("zb", [128, 16], BF16)
        eyep = sb("eyep_s", [16, 16])
        eyeg = sb("eyeg_s", [128, 16, 16])
        utail = sb("utail", [128, L, 8, 2])
        epsT = sb("epsT", [128, 1])
        scl = sb("scl", [128, 8, 2, NS])
        ncs_sb = sb("ncs_sb", [128, 8, 2, NS])
        ps = [es.enter_context(nc.psum_tensor(f"ps{i}", [128, 512], F32)) for i in range(8)]
        psb = [ps[i][:, :].bitcast(BF16) for i in range(8)]
        BK = {i: [("ps", i)] for i in range(8)}
        B4 = BK[4]
        B7 = BK[7]
        bank_ctr = [0]

        def next_bank():
            b = bank_ctr[0] % 4
            bank_ctr[0] += 1
            return b

        for (dst, src, key) in [(ng, ng_d, "ng"), (fg, fg_d, "fg"), (cw, cw_d, "cw"), (gg, gg_d, "gg"),
                                (dq, dq_d, "dtab"), (dk, dk_d, "dtab2"), (maskT, mask_d, "mask"),
                                (c3, c3_d, "sg"), (eyep, eyep_d, "eyep"), (eyeg, eyeg_d, "eyeg")]:
            P.dma("sp", (lambda e, d=dst, s=src: e.dma_start(out=(d if key == "sg" else d[:]), in_=s)), W=[key])
        P.op("dve", lambda e: e.tensor_copy(out=identb[:, :], in_=c3[:, 0, :]), R=["sg"], W=["identb"])
        P.op("dve", lambda e: e.tensor_copy(out=onesb[:, :], in_=c3[:, 1, :]), R=["sg"], W=["onesb"])
        P.op("dve", lambda e: e.memset(zb[:, :], 0.0), W=["zb"])
        P.op("dve", lambda e: e.memset(utail[:], 0.0), W=["utail"])
        P.op("dve", lambda e: e.memset(epsT[:], 1e-6), W=["epsT"])

        def layer_specs(l):
            wi = w_in[l].rearrange("(kc p) c -> p kc c", p=128)
            wo = w_out[l].rearrange("(kc p) c -> p kc c", p=128)
            wg = w_pg[l].rearrange("(kc p) c -> p kc c", p=128)
            wp = w_ple[l].rearrange("(kc p) c -> p kc c", p=128)
            sp = []
            for base in (4096, 5120, 6144, 7168):
                for h in range(4):
                    sp.append([(wi[:, :, base + 256 * h: base + 256 * h + 256], 0, 16, 256)])
            for j in range(8):
                sp.append([(wi[:, :, j * 128:(j + 1) * 128], 0, 16, 128),
                           (wi[:, :, 1024 + j * 128:1024 + (j + 1) * 128], 128, 16, 128)])
                sp.append([(wi[:, :, 2048 + j * 128:2048 + (j + 1) * 128], 0, 16, 128),
                           (wi[:, :, 3072 + j * 128:3072 + (j + 1) * 128], 128, 16, 128)])
            for b in range(8):
                sp.append([(wo[:, :, 256 * b:256 * b + 256], 0, 16, 256)])
            for b in range(8):
                sp.append([(wg[:, :, 256 * b:256 * b + 256], 0, 16, 256),
                           (wp[:, :, 256 * b:256 * b + 256], -1, 2, 256)])
            return sp

        NBL = len(layer_specs(0))
        all_specs = []
        for p in range(NPASS):
            for l in range(L):
                all_specs.extend(layer_specs(l))
        wcache = [nc.dram_tensor(f"wcache{l}", [NBL, 128, WSLOT], BF16).ap() for l in range(L)]
        wstate = {"issued": 0, "cur": 0}

        def flush_store():
            pend = wstate.get("pend")
            if pend is not None:
                lz, blk, slot, nel, keys = pend
                P.dma("pool", (lambda e: e.dma_start(out=wcache[lz][blk, :, 0:nel], in_=wb[slot][:, 0:nel])),
                      R=keys, W=[("wc", lz, blk)])
                wstate["pend"] = None

        def issue_loads(upto):
            while wstate["issued"] <= min(upto, len(all_specs) - 1):
                i = wstate["issued"]
                slot = i % NWS
                pz = i // (NBL * L)
                lz = (i // NBL) % L
                blk = i % NBL
                spec = all_specs[i]
                nel = 4608 if any(off < 0 for (_, off, _, _) in spec) else 4096
                keys = [("wb", slot, pi) for pi in range(len(spec))]
                cp = CACHE_PLAN[blk % 6]
                can_cache = cp < NPASS - 1
                if pz > cp and can_cache:
                    flush_store()
                    P.dma("pool", (lambda e, lz=lz, blk=blk, slot=slot, nel=nel: e.dma_start(
                        out=wb[slot][:, 0:nel], in_=wcache[lz][blk, :, 0:nel])), R=[("wc", lz, blk)], W=keys)
                else:
                    for pi, (src, off, nk, ncol) in enumerate(spec):
                        if off >= 0:
                            dst = wb[slot][:, 0:4096].rearrange("p (k c) -> p k c", c=256)[:, :, off:off + ncol]
                        else:
                            dst = wb[slot][:, 4096:4608].rearrange("p (k c) -> p k c", c=256)
                        P.dma("pool", (lambda e, d=dst, s=src: e.dma_start(out=d, in_=s)), W=[("wb", slot, pi)])
                    flush_store()
                    if pz == cp and can_cache:
                        wstate["pend"] = (lz, blk, slot, nel, keys)
                wstate["issued"] += 1

        def get_block():
            i = wstate["cur"]
            wstate["cur"] += 1
            issue_loads(i + NWS - 1)
            slot = i % NWS
            keys = [("wb", slot, pi) for pi in range(len(all_specs[i]))]
            wv = wb[slot][:, 0:4096].rearrange("p (k c) -> p k c", c=256)
            wpv = wb[slot][:, 4096:4608].rearrange("p (k c) -> p k c", c=256)
            return wv, wpv, keys

        for p in range(NPASS):
            SP = (NPASS - 1) if SAMPLE_LAST else 0
            T = NPR + (NS if p == SP else 0)
            TCH = [(0, 256), (256, T)] if p == SP else [(0, 512)]
            tok0 = p * NPR
            allx = [("xT", f) for f in range(16)]
            P.dma("sp", (lambda e, t=tok0: e.dma_start(out=xT[:, :, 0:NPR], in_=xp[:, :, t:t + NPR])), W=allx)
            if p == SP:
                P.dma("sp", (lambda e: e.dma_start(out=xT[:, :, NPR:T], in_=xs)), W=allx)
            P.dma("sp", (lambda e, p=p: e.dma_start(out=csT[:], in_=cs_d[:, :, p, :])), W=["cs"])

            def norm_finish(pre_banks, T=T, TCH=TCH):
                for ci, (t0, t1) in enumerate(TCH):
                    bk = pre_banks[ci]
                    n = t1 - t0
                    P.op("act", (lambda e, bk=bk, n=n, t0=t0, t1=t1: e.activation(
                        out=rstd[:, t0:t1], in_=ps[bk][:, 0:n], func=AF.Sqrt, bias=epsT[:, 0:1])),
                        R=[("ps", bk), "epsT"], W=["rstd"])
                    P.op("dve", (lambda e, t0=t0, t1=t1: e.reciprocal(out=rstd[:, t0:t1], in_=rstd[:, t0:t1])),
                         R=["rstd"], W=["rstd"])

            def sq_accum(f, T=T, TCH=TCH):
                P.op("act", (lambda e: e.activation(out=ymT[:, f, 0:T], in_=xT[:, f, 0:T], func=AF.Square)),
                     R=[("xT", f)], W=[("ym", f)])
                sq_pend.append(f)

            def sq_flush(keep, T=T, TCH=TCH):
                while len(sq_pend) > keep:
                    sq_mm(sq_pend.pop(0))

            def sq_mm(f, T=T, TCH=TCH):
                for ci, (t0, t1) in enumerate(TCH):
                    P.op("pe", (lambda e, ci=ci, t0=t0, t1=t1: e.matmul(
                        ps[4 + ci][:, 0:t1 - t0], onesb[:, :], ymT[:, f, t0:t1], start=(f == 0), stop=(f == 15))),
                        R=[("ym", f), "onesb"], W=[("ps", 4 + ci)])

            def norm_stats(T=T, TCH=TCH):
                if state_pre[0]:
                    state_pre[0] = False
                    norm_finish([4, 5])
                    return
                allym = [("ym", f) for f in range(16)]
                P.op("act", lambda e: e.activation(out=ymT[:, :, 0:T], in_=xT[:, :, 0:T], func=AF.Square),
                     R=[("xT", f) for f in range(16)], W=allym)
                for (t0, t1) in TCH:
                    bk = next_bank()
                    n = t1 - t0
                    for kc in range(16):
                        P.op("pe", (lambda e, bk=bk, n=n, kc=kc, t0=t0, t1=t1: e.matmul(
                            ps[bk][:, 0:n], onesb[:, :], ymT[:, kc, t0:t1], start=(kc == 0), stop=(kc == 15))),
                            R=[("ym", kc), "onesb"], W=[("ps", bk)])
                    P.op("act", (lambda e, bk=bk, n=n, t0=t0, t1=t1: e.activation(
                        out=rstd[:, t0:t1], in_=ps[bk][:, 0:n], func=AF.Sqrt, bias=epsT[:, 0:1])),
                        R=[("ps", bk), "epsT"], W=["rstd"])
                    P.op("dve", (lambda e, t0=t0, t1=t1: e.reciprocal(out=rstd[:, t0:t1], in_=rstd[:, t0:t1])),
                         R=["rstd"], W=["rstd"])

            state_pre = [False]
            sq_pend = []
            for l in range(L):
                P.dma("pool", (lambda e, l=l, t=tok0: e.dma_start(out=pT[:, :, 0:NPR], in_=pp[l][:, :, t:t + NPR])),
                      W=["pT"])
                if p == SP:
                    P.dma("pool", (lambda e, l=l: e.dma_start(out=pT[:, :, NPR:T], in_=psm[l])), W=["pT"])
                    P.dma("sp", (lambda e, l=l: e.dma_start(out=scl[:], in_=scT[l])), W=["scl"])
                if p == 0:
                    P.op("dve", lambda e: e.memset(S32[:], 0.0), W=["S32"])
                    P.op("dve", lambda e: e.memset(Sbf[:], 0.0), W=["Sbf"])
                else:
                    P.dma("sp", (lambda e, l=l: e.dma_start(out=S32[:], in_=nrp[l])), R=[("nrp", l)], W=["S32"])
                    P.op("act", lambda e: e.activation(out=Sbf[:], in_=S32[:], func=AF.Copy), R=["S32"], W=["Sbf"])

                norm_stats()
                for kc in range(16):
                    P.op("dve", (lambda e, kc=kc, l=l: e.scalar_tensor_tensor(
                        out=hT[:, kc, 0:T], in0=xT[:, kc, 0:T], scalar=ng[:, l, kc:kc + 1], in1=rstd[:, 0:T],
                        op0=ALU.mult, op1=ALU.mult)), R=[("xT", kc), "rstd", "ng"], W=[("hT", kc)])

                def fm_block(nkc, lhs_fn, rhs_fn, rkeys_fn, wkeys, evac, ncc=2):
                    for cc in range(ncc):
                        for ci, (t0, t1) in enumerate(TCH):
                            bk = next_bank()
                            n = t1 - t0
                            for kc in range(nkc):
                                P.op("pe", (lambda e, bk=bk, n=n, kc=kc, cc=cc, t0=t0, t1=t1: e.matmul(
                                    ps[bk][:, 0:n], lhs_fn(kc, cc), rhs_fn(kc, t0, t1),
                                    start=(kc == 0), stop=(kc == nkc - 1))),
                                    R=wkeys + rkeys_fn(kc), W=[("ps", bk)])
                            evac(cc, ci, t0, t1, bk, n)

                hkeys = lambda kc: [("hT", kc)]
                hrhs = lambda kc, t0, t1: hT[:, kc, t0:t1]

                if CFG["stage"] < 2:
                    continue
                for which, (dstT, tbl, tkey, sscale) in enumerate([(qT, dq, "dtab", 1.0), (kT, dk, "dtab2", 1.0 / 16)]):
                    dname = "q" if which == 0 else "k"
                    for h in range(4):
                        wv, wpv, wk = get_block()

                        def evac_qk(cc, ci, t0, t1, bk, n, h=h, tbl=tbl, tkey=tkey, sscale=sscale):
                            npr = min(t1, NPR) - t0
                            r = npr // 128
                            P.op("dve", (lambda e: e.tensor_tensor(
                                out=raw[:, cc, t0:t0 + npr].rearrange("p (r l) -> p r l", l=128),
                                in0=ps[bk][:, 0:npr].rearrange("p (r l) -> p r l", l=128),
                                in1=tbl[:, h:h + 1, :].broadcast_to([128, r, 128]), op=ALU.mult)),
                                R=[("ps", bk), tkey], W=[("raw", cc)])
                            if t1 > NPR:
                                P.op("dve", (lambda e: e.tensor_scalar_mul(out=raw[:, cc, NPR:t1], in0=ps[bk][:, npr:n],
                                                                           scalar1=sscale)),
                                     R=[("ps", bk)], W=[("raw", cc)])

                        fm_block(16, (lambda kc, cc, wv=wv: wv[:, kc, cc * 128:(cc + 1) * 128]), hrhs, hkeys, wk, evac_qk)
                        x1 = raw[:, 0, 0:T]
                        x2 = raw[:, 1, 0:T]
                        C = csT[:, 0, 0:T]
                        Sn = csT[:, 1, 0:T]
                        tt = [r_[:, 0:T] for r_ in rt]
                        for (o_, a_, b_, ok, ak) in [(tt[0], x1, C, RTK[0], ("raw", 0)), (tt[1], x2, Sn, RTK[1], ("raw", 1)),
                                                     (tt[2], x1, Sn, RTK[2], ("raw", 0)), (tt[3], x2, C, RTK[3], ("raw", 1))]:
                            P.op("dve", (lambda e, o_=o_, a_=a_, b_=b_: e.tensor_tensor(out=o_, in0=a_, in1=b_, op=ALU.mult)),
                                 R=[ak, "cs"], W=[ok])
                        P.op("dve", (lambda e, h=h, dstT=dstT: e.tensor_tensor(out=dstT[:, 2 * h, 0:T], in0=tt[0], in1=tt[1], op=ALU.subtract)),
                             R=[RTK[0], RTK[1]], W=[(dname, 2 * h)])
                        P.op("dve", (lambda e, h=h, dstT=dstT: e.tensor_tensor(out=dstT[:, 2 * h + 1, 0:T], in0=tt[2], in1=tt[3], op=ALU.add)),
                             R=[RTK[2], RTK[3]], W=[(dname, 2 * h + 1)])

                if CFG["stage"] < 3:
                    continue
                tiles = [(i, i * 128, 128) for i in range(4)] + ([(4, NPR, NS)] if p == SP else [])
                for h in range(4):
                    wv, wpv, wk = get_block()
                    for (ti, tk0, ntk) in tiles:
                        bk = next_bank()
                        for kc in range(16):
                            P.op("pe", (lambda e, bk=bk, kc=kc, tk0=tk0, ntk=ntk, wv=wv: e.matmul(
                                ps[bk][0:ntk, 0:256], hT[:, kc, tk0:tk0 + ntk], wv[:, kc, :],
                                start=(kc == 0), stop=(kc == 15))), R=wk + [("hT", kc)], W=[("ps", bk)])
                        P.op("act", (lambda e, bk=bk, ti=ti, ntk=ntk, h=h: e.activation(
                            out=v_tok[0:ntk, ti, h * 256:(h + 1) * 256], in_=ps[bk][0:ntk, 0:256], func=AF.Copy)),
                            R=[("ps", bk)], W=[("v", ti, h)])

                if CFG["stage"] < 4:
                    continue
                for h in range(4):
                    wv, wpv, wk = get_block()

                    def evac_g(cc, ci, t0, t1, bk, n, h=h):
                        f = 8 + 2 * h + cc
                        P.op("act", (lambda e: e.activation(out=ymT[:, f, t0:t1], in_=ps[bk][:, 0:n], func=AF.Silu)),
                             R=[("ps", bk)], W=[("ym", f)])
                        P.op("dve", (lambda e: e.tensor_scalar_mul(out=ymT[:, f, t0:t1], in0=ymT[:, f, t0:t1],
                                                                   scalar1=gg[:, l, f - 8:f - 7])), R=[("ym", f), "gg"], W=[("ym", f)])
                    fm_block(16, (lambda kc, cc, wv=wv: wv[:, kc, cc * 128:(cc + 1) * 128]), hrhs, hkeys, wk, evac_g)

                YG = [("ym", 8 + i_) for i_ in range(8)]

                def post_norm(ntok, psOs, okeys):
                    for h in range(4):
                        P.op("act", (lambda e, h=h: e.activation(out=junk[0:ntok, 0:256], in_=psOs[h], func=AF.Identity,
                                                            accum_out=st4[0:ntok, 0, h:h + 1])), R=okeys[h], W=["sg", "st4"])
                        P.op("act", (lambda e, h=h: e.activation(out=junk[0:ntok, 256:512], in_=psOs[h], func=AF.Square,
                                                            accum_out=st4[0:ntok, 1, h:h + 1])), R=okeys[h], W=["sg", "st4"])
                    s_ = st4
                    P.op("dve", (lambda e: e.tensor_scalar_mul(out=s_[0:ntok, 2:4, :], in0=s_[0:ntok, 0:2, :], scalar1=1.0 / 256)),
                         R=["st4"], W=["st4"])
                    P.op("dve", (lambda e: e.tensor_tensor(out=s_[0:ntok, 4, :], in0=s_[0:ntok, 2, :], in1=s_[0:ntok, 2, :], op=ALU.mult)),
                         R=["st4"], W=["st4"])
                    P.op("dve", (lambda e: e.tensor_tensor(out=s_[0:ntok, 5, :], in0=s_[0:ntok, 3, :], in1=s_[0:ntok, 4, :], op=ALU.subtract)),
                         R=["st4"], W=["st4"])
                    P.op("act", (lambda e: e.activation(out=s_[0:ntok, 6, :], in_=s_[0:ntok, 5, :], func=AF.Sqrt,
                                                        bias=epsT[0:ntok, 0:1])), R=["st4", "epsT"], W=["st4"])
                    P.op("dve", (lambda e: e.reciprocal(out=s_[0:ntok, 6, :], in_=s_[0:ntok, 6, :])), R=["st4"], W=["st4"])
                    P.op("dve", (lambda e: e.scalar_tensor_tensor(out=s_[0:ntok, 7, :], in0=s_[0:ntok, 2, :], scalar=-1.0,
                                                                  in1=s_[0:ntok, 6, :], op0=ALU.mult, op1=ALU.mult)),
                         R=["st4"], W=["st4"])
                    for h in range(4):
                        P.op("act", (lambda e, h=h: e.activation(out=on4[0:ntok, h, :], in_=psOs[h], func=AF.Identity,
                                                            scale=s_[0:ntok, 6, h:h + 1], bias=s_[0:ntok, 7, h:h + 1])),
                             R=okeys[h] + ["st4"], W=["on4"])

                def post_tr(ntok, c0, ob):
                    for half in range(2):
                        for h in range(4):
                            P.op("pe", (lambda e, half=half, h=h: e.transpose(
                                psb[ob][:, half * 512 + h * 128:half * 512 + h * 128 + ntok],
                                on4[0:ntok, h, half * 128:(half + 1) * 128], identb[0:ntok, 0:ntok])),
                                R=["on4", "identb"], W=[("ps", ob)])
                    for half in range(2):
                        yv = ymT[:, 8 + half:16:2, c0:c0 + ntok]
                        pv = psb[ob][:, half * 512:(half + 1) * 512].rearrange("p (h t) -> p h t", t=128)[:, :, 0:ntok]
                        P.op("dve", (lambda e, yv=yv, pv=pv: e.tensor_tensor(out=yv, in0=pv, in1=yv, op=ALU.mult)),
                             R=[("ps", ob)] + YG, W=YG)

                def sample_prep():
                    for hh in range(8):
                        P.op("pe", (lambda e, hh=hh: e.transpose(psb[7][0:NS, hh * 128:(hh + 1) * 128], kT[:, hh, NPR:T], identb[:, :])),
                             R=[("k", hh), "identb"], W=B7)
                    P.op("dve", lambda e: e.tensor_copy(out=ks_tok[:, :], in_=psb[7][0:NS, 0:1024]), R=B7, W=["ks_tok"])
                    for h in range(4):
                        for half in range(2):
                            P.op("pe", (lambda e, h=h, half=half: e.matmul(
                                ps[4][0:NS, h * 16:(h + 1) * 16], kT[:, 2 * h + half, NPR:T], qT[:, 2 * h + half, NPR:T],
                                start=(half == 0), stop=(half == 1))), R=[("k", 2 * h + half), ("q", 2 * h + half)], W=B4)
                    P.op("dve", lambda e: e.tensor_tensor(
                        out=sTs[:, :, :], in0=ps[4][0:NS, 0:64].rearrange("p (h t) -> p h t", t=16),
                        in1=eyep[:, :].unsqueeze(1).broadcast_to([NS, 4, 16]), op=ALU.mult), R=B4 + ["eyep"], W=["sTs"])
                    for hh in range(8):
                        P.op("dve", (lambda e, hh=hh: e.scalar_tensor_tensor(
                            out=Qm[:, hh, :, :], in0=qT[:, hh, NPR:T].unsqueeze(2).broadcast_to([128, 16, 16]),
                            scalar=GAM[hh // 2], in1=eyeg[:, :, :], op0=ALU.mult, op1=ALU.mult)),
                            R=[("q", hh), "eyeg"], W=[("Qm", hh)])
                    for bkk in (4, 5):
                        P.op("pe", (lambda e, bkk=bkk: e.matmul(ps[bkk][0:NS, 0:512], zb[:, 0:NS], hT[:, 0, 0:512],
                                                                start=True, stop=False, skip_group_check=True)),
                             R=["zb", ("hT", 0)], W=BK[bkk])
                    for h in range(4):
                        P.op("pe", (lambda e, h=h: e.matmul(
                            ps[4 + h // 2][0:NS, (h % 2) * 256:(h % 2) * 256 + 256], sTs[:, h, :],
                            v_tok[0:NS, 4, h * 256:(h + 1) * 256], start=False, stop=False, skip_group_check=True)),
                            R=["sTs"] + [("v", 4, h)], W=BK[4 + h // 2])

                def job_load(jb):
                    b, hp = jb // 2, jb % 2
                    sl = jb % 4
                    src = sret[l, b, 2 * hp:2 * hp + 2].rearrange("h (hf p) e -> p (h hf) e", p=128)
                    P.dma("sp", (lambda e: e.dma_start(out=Ss[sl][:], in_=src)), W=[("Ss", sl)])

                def sample_token(b):
                    vs = b % 2
                    P.op("dve", (lambda e: e.tensor_scalar_mul(out=Vm[vs][:, :], in0=v_tok[0:NS, 4, :], scalar1=eyep[:, b:b + 1])),
                         R=[("v", 4, h_) for h_ in range(4)] + ["eyep"], W=[("Vm", vs)])
                    for hp in range(2):
                        jb = 2 * b + hp
                        sl = jb % 4
                        if jb + 2 < 2 * NS:
                            job_load(jb + 2)
                        for q4 in range(4):
                            hh = 4 * hp + q4
                            h = hh // 2
                            P.op("pe", (lambda e, hh=hh, h=h, q4=q4: e.matmul(
                                ps[4 + hp][0:NS, (h % 2) * 256:(h % 2) * 256 + 256], Qm[:, hh, b, :], Ss[sl][:, q4, :],
                                start=False, stop=False, skip_group_check=True)),
                                R=[("Qm", hh), ("Ss", sl)], W=BK[4 + hp])
                        for hi in range(2):
                            h = 2 * hp + hi
                            bk = 6 + hi
                            for half in range(2):
                                P.op("pe", (lambda e, h=h, half=half, bk=bk: e.matmul(
                                    ps[bk][:, half * 256:(half + 1) * 256],
                                    ks_tok[:, h * 256 + half * 128:h * 256 + (half + 1) * 128],
                                    Vm[vs][:, h * 256:(h + 1) * 256], start=True, stop=True)),
                                    R=["ks_tok", ("Vm", vs)], W=BK[bk])
                            P.op("dve", (lambda e, h=h, hi=hi, bk=bk: e.scalar_tensor_tensor(
                                out=Ss[sl][:, 2 * hi:2 * hi + 2, :], in0=Ss[sl][:, 2 * hi:2 * hi + 2, :], scalar=GAM[h],
                                in1=ps[bk][:, :].rearrange("p (a e) -> p a e", e=256), op0=ALU.mult, op1=ALU.add)),
                                R=BK[bk] + [("Ss", sl)], W=[("Ss", sl)])
                        dst = nrs[l, b, 2 * hp:2 * hp + 2].rearrange("h (hf p) e -> p (h hf) e", p=128)
                        P.dma("sp", (lambda e, dst=dst: e.dma_start(out=dst, in_=Ss[sl][:])), R=[("Ss", sl)])

                def ret_R1(c):
                    tk = slice(c * 128, (c + 1) * 128)
                    kslot = c % 2
                    for h in range(4):
                        for half in range(2):
                            P.op("pe", (lambda e, h=h, half=half: e.transpose(
                                psb[4][:, h * 256 + half * 128:h * 256 + (half + 1) * 128], kT[:, 2 * h + half, tk], identb[:, :])),
                                R=[("k", 2 * h + half), "identb"], W=[("ps", 4)])
                    for h in range(4):
                        P.op("act", (lambda e, h=h: e.activation(
                            out=k_tok[kslot][:, h * 256:(h + 1) * 256], in_=psb[4][:, h * 256:(h + 1) * 256], func=AF.Copy, scale=G128[h])),
                            R=[("ps", 4)], W=[("ktok", kslot)])
                    for h in range(4):
                        for half in range(2):
                            P.op("pe", (lambda e, h=h, half=half: e.matmul(
                                ps[5][:, h * 128:(h + 1) * 128], kT[:, 2 * h + half, tk], qT[:, 2 * h + half, tk],
                                start=(half == 0), stop=(half == 1))),
                                R=[("k", 2 * h + half), ("q", 2 * h + half)], W=[("ps", 5)])
                    P.op("dve", (lambda e: e.tensor_tensor(
                        out=sT4[:, :, :], in0=ps[5][:, :].rearrange("p (h t) -> p h t", t=128),
                        in1=maskT[:, :].unsqueeze(1).broadcast_to([128, 4, 128]), op=ALU.mult)),
                        R=[("ps", 5), "mask"], W=["sT4"])

                def ret_R2(c):
                    tk = slice(c * 128, (c + 1) * 128)
                    psOs = []
                    for h in range(4):
                        ob = 6 + h // 2
                        reg = (h % 2) * 256
                        psOs.append(ps[ob][:, reg:reg + 256])
                        P.op("pe", (lambda e, h=h, ob=ob, reg=reg: e.matmul(
                            ps[ob][:, reg:reg + 256], sT4[:, h, :], v_tok[:, c, h * 256:(h + 1) * 256], start=True, stop=False)),
                            R=["sT4", ("v", c, h)], W=[("ps", ob)])
                        for half in range(2):
                            P.op("pe", (lambda e, h=h, half=half, ob=ob, reg=reg: e.matmul(
                                ps[ob][:, reg:reg + 256], qT[:, 2 * h + half, tk], Sbf[:, 2 * h + half, :],
                                start=False, stop=(half == 1))), R=[("q", 2 * h + half), "Sbf"], W=[("ps", ob)])
                    post_norm(128, psOs, [BK[6 + h // 2] for h in range(4)])

                def ret_KV(c, heads, banks):
                    kslot = c % 2
                    for h, bk in zip(heads, banks):
                        for half in range(2):
                            P.op("pe", (lambda e, h=h, half=half, bk=bk: e.matmul(
                                ps[bk][:, half * 256:(half + 1) * 256],
                                k_tok[kslot][:, h * 256 + half * 128:h * 256 + (half + 1) * 128],
                                v_tok[:, c, h * 256:(h + 1) * 256], start=True, stop=True)),
                                R=[("ktok", kslot), ("v", c, h)], W=[("ps", bk)])
                    for h, bk in zip(heads, banks):
                        P.op("dve", (lambda e, h=h, bk=bk: e.scalar_tensor_tensor(
                            out=S32[:, 2 * h:2 * h + 2, :], in0=S32[:, 2 * h:2 * h + 2, :], scalar=G128[h],
                            in1=ps[bk][:, :].rearrange("p (a e) -> p a e", e=256), op0=ALU.mult, op1=ALU.add)),
                            R=[("ps", bk), "S32"], W=["S32"])
                        P.op("act", (lambda e, h=h: e.activation(out=Sbf[:, 2 * h:2 * h + 2, :], in_=S32[:, 2 * h:2 * h + 2, :],
                                                                 func=AF.Copy)), R=["S32"], W=["Sbf"])

                ret_stages = []
                if CFG["stage"] >= 7:
                    for c in range(4):
                        if c == 0:
                            ret_stages.append([lambda c=c: ret_R1(c)])
                        ret_stages.append([lambda c=c: ret_R2(c)])
                        ret_stages.append([lambda c=c: post_tr(128, c * 128, 4), lambda c=c: ret_KV(c, [0, 1], [6, 7])])
                        last = [lambda c=c: ret_KV(c, [2, 3], [6, 7])]
                        if c < 3:
                            last.append(lambda c=c: ret_R1(c + 1))
                        ret_stages.append(last)

                def run_stage():
                    if ret_stages:
                        for fn_ in ret_stages.pop(0):
                            fn_()

                def hook(j, second):
                    if p != SP:
                        run_stage()
                        return
                    if j < 4:
                        hk = 2 * j + second
                        if hk == 0:
                            sample_prep()
                            job_load(0)
                            job_load(1)
                        sample_token(2 * hk)
                        sample_token(2 * hk + 1)
                        if hk == 7:
                            post_norm(NS, [ps[4 + h // 2][0:NS, (h % 2) * 256:(h % 2) * 256 + 256] for h in range(4)],
                                      [BK[4 + h // 2] for h in range(4)])
                            post_tr(NS, NPR, 6)
                    else:
                        run_stage()
                        run_stage()

                if CFG["stage"] < 5:
                    continue
                for j in range(8):
                    wv, wpv, wk = get_block()

                    def evac_a(cc, ci, t0, t1, bk, n):
                        dst = tb if cc == 0 else tc
                        P.op("act", (lambda e: e.activation(out=dst[:, t0:t1], in_=ps[bk][:, 0:n], func=AF.Copy)),
                             R=[("ps", bk)], W=["tb" if cc == 0 else "tc"])
                    fm_block(16, (lambda kc, cc, wv=wv: wv[:, kc, cc * 128:(cc + 1) * 128]), hrhs, hkeys, wk, evac_a)
                    hook(j, 0)
                    wv, wpv, wk = get_block()

                    def evac_b(cc, ci, t0, t1, bk, n, j=j):
                        if cc == 0:
                            P.op("dve", (lambda e: e.tensor_tensor(out=uext[:, 2 + t0:2 + t1], in0=ps[bk][:, 0:n],
                                                                   in1=tc[:, t0:t1], op=ALU.mult)),
                                 R=[("ps", bk), "tc"], W=["u"])
                        else:
                            P.op("act", (lambda e: e.activation(out=ymT[:, j, t0:t1], in_=ps[bk][:, 0:n], func=AF.Silu)),
                                 R=[("ps", bk)], W=[("ym", j)])
                    fm_block(16, (lambda kc, cc, wv=wv: wv[:, kc, cc * 128:(cc + 1) * 128]), hrhs, hkeys, wk, evac_b)
                    w0 = cw[:, l, j, 0:1]
                    w1 = cw[:, l, j, 1:2]
                    w2 = cw[:, l, j, 2:3]
                    P.op("dve", (lambda e, j=j: e.tensor_copy(out=uext[:, 0:2], in_=utail[:, l, j, :])), R=["utail"], W=["u"])
                    P.op("dve", (lambda e, w0=w0: e.tensor_scalar_mul(out=acc[:, 0:NPR], in0=uext[:, 0:NPR], scalar1=w0)), R=["u", "cw"], W=["acc"])
                    P.op("dve", (lambda e, w1=w1: e.scalar_tensor_tensor(out=acc[:, 0:NPR], in0=uext[:, 1:NPR + 1], scalar=w1,
                                                                         in1=acc[:, 0:NPR], op0=ALU.mult, op1=ALU.add)),
                         R=["u", "acc"], W=["acc"])
                    P.op("dve", (lambda e, w2=w2: e.scalar_tensor_tensor(out=acc[:, 0:NPR], in0=uext[:, 2:NPR + 2], scalar=w2,
                                                                         in1=acc[:, 0:NPR], op0=ALU.mult, op1=ALU.add)),
                         R=["u", "acc"], W=["acc"])
                    P.op("dve", (lambda e: e.tensor_tensor(out=acc[:, 0:NPR], in0=acc[:, 0:NPR], in1=tb[:, 0:NPR], op=ALU.mult)),
                         R=["acc", "tb"], W=["acc"])
                    P.op("dve", (lambda e, j=j: e.tensor_tensor(out=ymT[:, j, 0:NPR], in0=acc[:, 0:NPR], in1=ymT[:, j, 0:NPR],
                                                                op=ALU.mult)), R=["acc", ("ym", j)], W=[("ym", j)])
                    P.op("dve", (lambda e, j=j: e.tensor_copy(out=utail[:, l, j, :], in_=uext[:, NPR:NPR + 2])), R=["u"], W=["utail"])
                    if p == SP:
                        us = uext[:, 2 + NPR:2 + T]
                        sa = acc[:, NPR:T]
                        P.op("dve", (lambda e, j=j, w0=w0: e.tensor_scalar_mul(out=sa, in0=scl[:, j, 0, :], scalar1=w0)), R=["scl", "cw", "acc"], W=["acc"])
                        P.op("dve", (lambda e, j=j, w1=w1: e.scalar_tensor_tensor(out=sa, in0=scl[:, j, 1, :], scalar=w1, in1=sa,
                                                                                  op0=ALU.mult, op1=ALU.add)), R=["scl", "acc"], W=["acc"])
                        P.op("dve", (lambda e, w2=w2: e.scalar_tensor_tensor(out=sa, in0=us, scalar=w2, in1=sa,
                                                                             op0=ALU.mult, op1=ALU.add)), R=["u", "acc"], W=["acc"])
                        P.op("dve", (lambda e: e.tensor_tensor(out=sa, in0=sa, in1=tb[:, NPR:T], op=ALU.mult)), R=["acc", "tb"], W=["acc"])
                        P.op("dve", (lambda e, j=j: e.tensor_tensor(out=ymT[:, j, NPR:T], in0=sa, in1=ymT[:, j, NPR:T], op=ALU.mult)),
                             R=["acc", ("ym", j)], W=[("ym", j)])
                        P.op("dve", (lambda e, j=j: e.tensor_copy(out=ncs_sb[:, j, 0, :], in_=scl[:, j, 1, :])), R=["scl"], W=["ncs_sb"])
                        P.op("dve", (lambda e, j=j: e.tensor_copy(out=ncs_sb[:, j, 1, :], in_=us)), R=["u"], W=["ncs_sb"])
                    hook(j, 1)
                if p == SP:
                    P.dma("sp", (lambda e, l=l: e.dma_start(out=ncs[l], in_=ncs_sb[:])), R=["ncs_sb"])
                if p == NPASS - 1:
                    P.dma("sp", (lambda e, l=l: e.dma_start(out=ncp[l], in_=utail[:, l, :, :])), R=["utail"])

                if CFG["stage"] < 7:
                    continue

                while ret_stages:
                    run_stage()
                P.dma("sp", (lambda e, l=l: e.dma_start(out=nrp[l], in_=S32[:])), R=["S32"], W=[("nrp", l)])

                if CFG["stage"] < 8:
                    continue
                ykeys = lambda kc: [("ym", kc)]
                yrhs = lambda kc, t0, t1: ymT[:, kc, t0:t1]
                for b in range(8):
                    wv, wpv, wk = get_block()

                    def evac_o(cc, ci, t0, t1, bk, n, b=b):
                        f = 2 * b + cc
                        P.op("dve", (lambda e: e.tensor_tensor(out=xT[:, f, t0:t1], in0=ps[bk][:, 0:n], in1=xT[:, f, t0:t1], op=ALU.add)),
                             R=[("ps", bk), ("xT", f)], W=[("xT", f)])
                    fm_block(16, (lambda kc, cc, wv=wv: wv[:, kc, cc * 128:(cc + 1) * 128]), yrhs, ykeys, wk, evac_o)
                    for cc in range(2):
                        f = 2 * b + cc
                        P.op("act", (lambda e, f=f: e.activation(out=hT[:, f, 0:T], in_=xT[:, f, 0:T], func=AF.Copy)),
                             R=[("xT", f)], W=[("hT", f)])

                if CFG["stage"] < 9:
                    continue
                for b in range(8):
                    wv, wpv, wk = get_block()
                    for cc in range(2):
                        f = 2 * b + cc
                        for (t0, t1) in TCH:
                            n = t1 - t0
                            bp = next_bank()
                            for k2 in range(2):
                                P.op("pe", (lambda e, bp=bp, n=n, k2=k2, cc=cc, t0=t0, t1=t1, wpv=wpv: e.matmul(
                                    ps[bp][:, 0:n], wpv[:, k2, cc * 128:(cc + 1) * 128], pT[:, k2, t0:t1],
                                    start=(k2 == 0), stop=(k2 == 1))), R=wk + ["pT"], W=[("ps", bp)])
                            bg = next_bank()
                            for kc in range(16):
                                P.op("pe", (lambda e, bg=bg, n=n, kc=kc, cc=cc, t0=t0, t1=t1, wv=wv: e.matmul(
                                    ps[bg][:, 0:n], wv[:, kc, cc * 128:(cc + 1) * 128], hT[:, kc, t0:t1],
                                    start=(kc == 0), stop=(kc == 15))), R=wk + [("hT", kc)], W=[("ps", bg)])
                            P.op("act", (lambda e, bg=bg, n=n, t0=t0, t1=t1: e.activation(out=sg[:, t0:t1], in_=ps[bg][:, 0:n], func=AF.Sigmoid)),
                                 R=[("ps", bg)], W=["sg"])
                            P.op("dve", (lambda e, bp=bp, n=n, t0=t0, t1=t1: e.tensor_tensor(out=sg[:, t0:t1], in0=ps[bp][:, 0:n],
                                                                                             in1=sg[:, t0:t1], op=ALU.mult)),
                                 R=[("ps", bp), "sg"], W=["sg"])
                            P.op("dve", (lambda e, f=f, t0=t0, t1=t1: e.tensor_tensor(out=xT[:, f, t0:t1], in0=xT[:, f, t0:t1],
                                                                                      in1=sg[:, t0:t1], op=ALU.add)),
                                 R=["sg", ("xT", f)], W=[("xT", f)])
                        if PRE_NORM:
                            sq_flush(1)
                            sq_accum(f)
                if PRE_NORM and CFG["stage"] >= 9:
                    sq_flush(0)
                    state_pre[0] = True

            norm_stats()
            for kc in range(16):
                P.op("dve", (lambda e, kc=kc: e.scalar_tensor_tensor(
                    out=xT[:, kc, 0:T], in0=xT[:, kc, 0:T], scalar=fg[:, kc:kc + 1], in1=rstd[:, 0:T],
                    op0=ALU.mult, op1=ALU.mult)), R=[("xT", kc), "rstd", "fg"], W=[("xT", kc)])
            P.dma("sp", (lambda e, t=tok0: e.dma_start(out=yT[:, :, t:t + NPR], in_=xT[:, :, 0:NPR])), R=allx)
            if p == SP:
                P.dma("sp", (lambda e: e.dma_start(out=ysT, in_=xT[:, :, NPR:T])), R=allx)

        sems = {}
        for ename in ["pe", "act", "dve", "pool"]:
            sems[ename] = es.enter_context(nc.semaphore("s_" + ename))
        for q, n in P.nslot.items():
            for i in range(n):
                sems[("dma", q, i)] = es.enter_context(nc.semaphore(f"d_{q}_{i}"))
        for q, n in P.nslot.items():
            for i in range(n):
                if P.slot_uses[q][i] > 0:
                    P.lists["sp"].append(("wait", ("dma", q, i), 16 * P.slot_uses[q][i]))
        for ename in ["pe", "act", "dve"]:
            P.lists["sp"].append(("wait", ename, P.count[ename]))

        block = es.enter_context(nc.Block())

        def replay(name, e):
            for it in P.lists[name]:
                if it[0] == "wait":
                    e.wait_ge(sems[it[1]], it[2])
                elif it[0] == "op":
                    nm, a, k = it[1]
                    getattr(e, nm)(*a, **k).then_inc(sems[name], 1)
                else:
                    nm, a, k = it[1]
                    getattr(e, nm)(*a, **k).then_inc(sems[it[2]], 16)

        @block.tensor
        def _(e):
            replay("pe", e)

        @block.scalar
        def _(e):
            replay("act", e)

        @block.vector
        def _(e):
            replay("dve", e)

        @block.gpsimd
        def _(e):
            replay("pool", e)

        @block.sync
        def _(e):
            replay("sp", e)
    return nc


def _tables():
    half = 128
    inv = np.power(np.float32(10000.0), -(np.arange(half, dtype=np.float32) / np.float32(half))).astype(np.float32)
    cs = np.zeros((128, 2, NPASS, TMAX), np.float32)
    for p in range(NPASS):
        pos = np.concatenate([np.arange(p * NPR, (p + 1) * NPR), np.full(NS, 16384)]).astype(np.float32)
        ang = (pos[None, :] * inv[:, None]).astype(np.float32).astype(np.float64)
        cs[:, 0, p, :] = np.cos(ang)
        cs[:, 1, p, :] = np.sin(ang)
    idx = np.arange(128, dtype=np.float64)
    dq = np.zeros((128, 4, 128), np.float32)
    dk = np.zeros((128, 4, 128), np.float32)
    eyeg = np.zeros((128, 16, 16), np.float32)
    eyeg[:] = np.eye(16)[None]
    for h in range(4):
        dq[:, h, :] = (GAM[h] ** (idx + 1.0))[None, :]
        dk[:, h, :] = (GAM[h] ** (-(idx + 1.0)) / 16.0)[None, :]
    m = np.arange(128)
    maskT = (m[:, None] <= m[None, :]).astype(np.float32)
    c3 = np.zeros((128, 3, 128), np.float32)
    c3[:, 0, :] = np.eye(128)
    c3[:, 1, :] = 1.0 / 2048.0
    return dict(cs=cs, dq=dq, dk=dk, eyeg=eyeg, maskT=maskT, c3=c3, eyep=np.eye(16, dtype=np.float32))


def _fm(a):
    tok, d = a.shape
    return np.ascontiguousarray(a.reshape(tok, d // 128, 128).transpose(2, 1, 0))


def kernel(x_prompt, x_sample, state_conv, state_ret, p_prompt, p_sample,
           norm_g, w_in, conv_w, gn_g, w_out, w_pg, w_ple, final_g):
    f = lambda a: np.ascontiguousarray(np.asarray(a, dtype=np.float32))
    x_prompt, x_sample, state_conv, state_ret = f(x_prompt), f(x_sample), f(state_conv), f(state_ret)
    p_prompt, p_sample, norm_g, w_in, conv_w = f(p_prompt), f(p_sample), f(norm_g), f(w_in), f(conv_w)
    gn_g, w_out, w_pg, w_ple, final_g = f(gn_g), f(w_out), f(w_pg), f(w_ple), f(final_g)
    tabs = _tables()
    shared = dict(w_in=w_in, w_out=w_out, w_pg=w_pg, w_ple=w_ple,
                  ng=np.ascontiguousarray(norm_g.reshape(L, 16, 128).transpose(2, 0, 1)),
                  fg=np.ascontiguousarray(final_g.reshape(16, 128).T),
                  cw=np.ascontiguousarray(conv_w.reshape(L, 3, 8, 128).transpose(3, 0, 2, 1)),
                  gg=np.ascontiguousarray(gn_g.reshape(L, 8, 128).transpose(2, 0, 1)), **tabs)
    in_maps = []
    for c in range(NCORE):
        b = PMAP.get(c)
        s0 = c * NS
        m = dict(shared)
        if b is None:
            m["xp"] = np.zeros((128, 16, SEQ), np.float32)
            m["pp"] = np.zeros((L, 128, 2, SEQ), np.float32)
        else:
            m["xp"] = _fm(x_prompt[b])
            m["pp"] = np.stack([_fm(p_prompt[l, b]) for l in range(L)])
        m["xs"] = _fm(x_sample[s0:s0 + NS, 0, :])
        m["psm"] = np.stack([_fm(p_sample[l, s0:s0 + NS, 0, :]) for l in range(L)])
        sc = state_conv[:, s0:s0 + NS].reshape(L, NS, 2, 8, 128)
        m["scT"] = np.ascontiguousarray(sc.transpose(0, 4, 3, 2, 1))
        m["sret"] = np.ascontiguousarray(state_ret[:, s0:s0 + NS])
        in_maps.append(m)
    nc = build_nc()
    res = run_bass_kernel_spmd(nc, in_maps, core_ids=list(range(NCORE)))
    R = res.results
    y_prompt = np.zeros((4, SEQ, D), np.float32)
    y_sample = np.zeros((128, 1, D), np.float32)
    ncp_o = np.zeros((L, 4, 2, 1024), np.float32)
    nrp_o = np.zeros((L, 4, 4, 256, 256), np.float32)
    ncs_o = np.zeros((L, 128, 2, 1024), np.float32)
    nrs_o = np.zeros((L, 128, 4, 256, 256), np.float32)
    for c in range(NCORE):
        r = R[c]
        s0 = c * NS
        y_sample[s0:s0 + NS, 0, :] = np.asarray(r["ysT"]).transpose(2, 1, 0).reshape(NS, D)
        ncs_o[:, s0:s0 + NS] = np.asarray(r["ncs"]).transpose(0, 4, 3, 2, 1).reshape(L, NS, 2, 1024)
        nrs_o[:, s0:s0 + NS] = np.asarray(r["nrs"])
        c_b = PMAP.get(c)
        if c_b is not None:
            y_prompt[c_b] = np.asarray(r["yT"]).transpose(2, 1, 0).reshape(SEQ, D)
            ncp_o[:, c_b] = np.asarray(r["ncp"]).transpose(0, 3, 2, 1).reshape(L, 2, 1024)
            nrp_o[:, c_b] = np.asarray(r["nrp"]).reshape(L, 128, 4, 2, 256).transpose(0, 2, 3, 1, 4).reshape(L, 4, 256, 256)
    return (y_prompt, y_sample, ncp_o, nrp_o, ncs_o, nrs_o)
```
